# Optimizing a Trainium2 kernel written in Bass

```python
import math
import jax, jax.numpy as jnp
from jax import lax
import numpy as np

D_MODEL = 2048
BATCH = 2
SEQ = 16384
DEPTH = 2

GRID_W = 64
CTX_LEN = 256
N_MIXERS = 2
N_ATTN_LAYERS = (DEPTH + 1) // 2
N_SSM_LAYERS = DEPTH // 2
N_MOD = 9
N_NORMS = 6
EPS = 1e-6
NEG_INF = -1e30
N_HEADS = 16
N_KV_HEADS = 4
HEAD_DIM = D_MODEL // N_HEADS
GQA_GROUP = N_HEADS // N_KV_HEADS
WINDOW = 128
BLOCK = 128
QKV_DIM = (N_HEADS + 2 * N_KV_HEADS) * HEAD_DIM
ROPE_BASE = 10000.0
ROPE_AXIS_DIM = HEAD_DIM // 2
SSM_GROUP_CH = 16
SSM_GROUPS = D_MODEL // SSM_GROUP_CH
SSM_STATE = 64
SSM_CHUNK = 128
DT_MIN = 1e-3
DT_MAX = 1e-1
D_FF = 5632

kernel_name = "hybrid_swa_s5_macaron_dit"


def rmsnorm(t, g):
    tf = t.astype(jnp.float32)
    tf = tf * lax.rsqrt(jnp.mean(tf * tf, axis=-1, keepdims=True) + EPS)
    return (tf * g.astype(jnp.float32)).astype(t.dtype)


def pre_sublayer(t, g, shift, scale):
    return rmsnorm(t, g) * (1 + scale) + shift


def post_sublayer(t, out, g, gate, weight):
    return t + weight * gate * rmsnorm(out, g)


def swiglu(h, w_in, w_out):
    g, u = jnp.split(h @ w_in, 2, axis=-1)
    return (jax.nn.silu(g) * u) @ w_out


def axial_rope_tables(rows):
    row = jnp.repeat(jnp.arange(rows, dtype=jnp.float32), GRID_W)
    col = jnp.tile(jnp.arange(GRID_W, dtype=jnp.float32), rows)
    inv_freq = ROPE_BASE ** (-jnp.arange(0, ROPE_AXIS_DIM, 2, dtype=jnp.float32) / ROPE_AXIS_DIM)
    ang = jnp.concatenate([row[:, None] * inv_freq, col[:, None] * inv_freq], axis=-1)
    return jnp.cos(ang), jnp.sin(ang)


def apply_axial_rope(t, cos, sin):
    tf = t.astype(jnp.float32)
    half = ROPE_AXIS_DIM // 2
    outs = []
    for a in range(2):
        seg = tf[..., a * ROPE_AXIS_DIM:(a + 1) * ROPE_AXIS_DIM]
        cs = cos[:, None, a * half:(a + 1) * half]
        sn = sin[:, None, a * half:(a + 1) * half]
        x1, x2 = seg[..., :half], seg[..., half:]
        outs += [x1 * cs - x2 * sn, x2 * cs + x1 * sn]
    return jnp.concatenate(outs, axis=-1).astype(t.dtype)


def softmax_with_sink(logits, sink_logit):
    full = jnp.concatenate([logits, jnp.broadcast_to(sink_logit, logits.shape[:-1] + (1,))], axis=-1)
    return jax.nn.softmax(full, axis=-1)[..., :-1]


def band_blocks(t, n_blk):
    b = t.shape[0]
    tb = t.reshape(b, n_blk, BLOCK, t.shape[2], t.shape[3])
    tp = jnp.pad(tb, ((0, 0), (1, 1), (0, 0), (0, 0), (0, 0)))
    return jnp.concatenate([tp[:, :-2], tp[:, 1:-1], tp[:, 2:]], axis=2)


def band_mask(n_blk, seq_len):
    i = jnp.arange(BLOCK)[:, None]
    j = jnp.arange(3 * BLOCK)[None, :]
    rel = j - i
    kpos = (jnp.arange(n_blk)[:, None, None] - 1) * BLOCK + j[None]
    return (rel >= BLOCK - WINDOW) & (rel <= BLOCK + WINDOW) & (kpos >= 0) & (kpos < seq_len)


def windowed_gqa(h_lat, h_ctx, w_in, w_out, sink, cos, sin, need_ctx_out):
    b, seq_len, _ = h_lat.shape
    c_len = h_ctx.shape[1]
    n_blk = seq_len // BLOCK
    scale = HEAD_DIM ** -0.5
    splits = [N_HEADS * HEAD_DIM, (N_HEADS + N_KV_HEADS) * HEAD_DIM]
    q, k, v = jnp.split(h_lat @ w_in, splits, axis=-1)
    q = apply_axial_rope(q.reshape(b, seq_len, N_HEADS, HEAD_DIM), cos, sin) * scale
    k = apply_axial_rope(k.reshape(b, seq_len, N_KV_HEADS, HEAD_DIM), cos, sin)
    v = v.reshape(b, seq_len, N_KV_HEADS, HEAD_DIM)
    qc, kc, vc = jnp.split(h_ctx @ w_in, splits, axis=-1)
    kc = kc.reshape(b, c_len, N_KV_HEADS, HEAD_DIM)
    vc = vc.reshape(b, c_len, N_KV_HEADS, HEAD_DIM)
    sink_logit = sink.astype(jnp.float32).reshape(N_KV_HEADS, GQA_GROUP, 1, 1)
    qb = q.reshape(b, n_blk, BLOCK, N_KV_HEADS, GQA_GROUP, HEAD_DIM)
    kb, vb = band_blocks(k, n_blk), band_blocks(v, n_blk)
    s_band = jnp.einsum('bnqhgd,bnkhd->bnhgqk', qb, kb).astype(jnp.float32)
    s_band = jnp.where(band_mask(n_blk, seq_len)[None, :, None, None], s_band, NEG_INF)
    s_ctx = jnp.einsum('bnqhgd,bchd->bnhgqc', qb, kc).astype(jnp.float32)
    p = softmax_with_sink(jnp.concatenate([s_band, s_ctx], axis=-1), sink_logit).astype(v.dtype)
    nb = 3 * BLOCK
    o = (jnp.einsum('bnhgqk,bnkhd->bnqhgd', p[..., :nb], vb)
         + jnp.einsum('bnhgqc,bchd->bnqhgd', p[..., nb:], vc))
    out_lat = o.reshape(b, seq_len, D_MODEL) @ w_out
    out_ctx = None
    if need_ctx_out:
        qc = qc.reshape(b, c_len, N_KV_HEADS, GQA_GROUP, HEAD_DIM) * scale
        s = jnp.einsum('bqhgd,bkhd->bhgqk', qc, kc).astype(jnp.float32)
        pc = softmax_with_sink(s, sink_logit).astype(vc.dtype)
        out_ctx = jnp.einsum('bhgqk,bkhd->bqhgd', pc, vc).reshape(b, c_len, D_MODEL) @ w_out
    return out_lat, out_ctx


def zoh_discretise(a_re, a_im, log_dt, b_re, b_im):
    a_re, a_im = a_re.astype(jnp.float32), a_im.astype(jnp.float32)
    b_re, b_im = b_re.astype(jnp.float32), b_im.astype(jnp.float32)
    dt = jnp.exp(log_dt.astype(jnp.float32))[:, None]
    mag = jnp.exp(a_re * dt)
    ab_re, ab_im = mag * jnp.cos(a_im * dt), mag * jnp.sin(a_im * dt)
    den = a_re * a_re + a_im * a_im
    x_re, x_im = ab_re - 1.0, ab_im
    f_re = ((x_re * a_re + x_im * a_im) / den)[..., None]
    f_im = ((x_im * a_re - x_re * a_im) / den)[..., None]
    bb_re = f_re * b_re - f_im * b_im
    bb_im = f_re * b_im + f_im * b_re
    return ab_re, ab_im, bb_re, bb_im


def _ssm_combine(e_i, e_j):
    ai_re, ai_im, bi_re, bi_im = e_i
    aj_re, aj_im, bj_re, bj_im = e_j
    return (aj_re * ai_re - aj_im * ai_im,
            aj_re * ai_im + aj_im * ai_re,
            aj_re * bi_re - aj_im * bi_im + bj_re,
            aj_re * bi_im + aj_im * bi_re + bj_im)


def s5_scan(u, h0_re, h0_im, disc, c_re, c_im):
    ab_re, ab_im, bb_re, bb_im = disc
    t_len = u.shape[0]
    uc = u.reshape((t_len // SSM_CHUNK, SSM_CHUNK) + u.shape[1:])
    a_re = jnp.broadcast_to(ab_re, (SSM_CHUNK, 1) + ab_re.shape)
    a_im = jnp.broadcast_to(ab_im, (SSM_CHUNK, 1) + ab_im.shape)

    def step(carry, u_blk):
        h_re, h_im = carry
        bu_re = jnp.einsum('tbgc,gpc->tbgp', u_blk, bb_re)
        bu_im = jnp.einsum('tbgc,gpc->tbgp', u_blk, bb_im)
        bu_re = bu_re.at[0].add(ab_re * h_re - ab_im * h_im)
        bu_im = bu_im.at[0].add(ab_re * h_im + ab_im * h_re)
        _, _, s_re, s_im = lax.associative_scan(_ssm_combine, (a_re, a_im, bu_re, bu_im), axis=0)
        y = jnp.einsum('tbgp,gcp->tbgc', s_re, c_re) - jnp.einsum('tbgp,gcp->tbgc', s_im, c_im)
        return (s_re[-1], s_im[-1]), y

    (h_re, h_im), y = lax.scan(step, (h0_re, h0_im), uc)
    return y.reshape(u.shape), h_re, h_im


def s5_mixer(h_lat, h_ctx, w_in, a_re, a_im, log_dt, b_re, b_im, c_re, c_im, d_skip, w_glu, need_ctx_out):
    b = h_lat.shape[0]

    def to_groups(h):
        u = (h @ w_in).astype(jnp.float32)
        t_len = u.shape[1]
        return u, jnp.transpose(u.reshape(b, t_len, SSM_GROUPS, SSM_GROUP_CH), (1, 0, 2, 3))

    u_lat, ut_lat = to_groups(h_lat)
    u_ctx, ut_ctx = to_groups(h_ctx)
    zero = jnp.zeros((b, SSM_GROUPS, SSM_STATE), jnp.float32)
    ys_lat, ys_ctx = [], []
    for dr in range(2):
        disc = zoh_discretise(a_re[dr], a_im[dr], log_dt[dr], b_re[dr], b_im[dr])
        cr, ci = c_re[dr].astype(jnp.float32), c_im[dr].astype(jnp.float32)
        uc = ut_ctx[::-1] if dr == 1 else ut_ctx
        ul = ut_lat[::-1] if dr == 1 else ut_lat
        yc, hr, hi = s5_scan(uc, zero, zero, disc, cr, ci)
        yl, _, _ = s5_scan(ul, hr, hi, disc, cr, ci)
        ys_ctx.append(yc[::-1] if dr == 1 else yc)
        ys_lat.append(yl[::-1] if dr == 1 else yl)

    def readout(y_t, u):
        t_len = u.shape[1]
        y = jnp.transpose(y_t, (1, 0, 2, 3)).reshape(b, t_len, D_MODEL) + d_skip.astype(jnp.float32) * u
        g = jax.nn.gelu(y).astype(h_lat.dtype)
        val, gate = jnp.split(g @ w_glu, 2, axis=-1)
        return val * jax.nn.sigmoid(gate)

    out_lat = readout(ys_lat[0] + ys_lat[1], u_lat)
    out_ctx = readout(ys_ctx[0] + ys_ctx[1], u_ctx) if need_ctx_out else None
    return out_lat, out_ctx


def setup_inputs(seed: int = 0) -> dict:
    key = jax.random.key(seed)
    ks = jax.random.split(key, 24)
    f32 = jnp.float32
    d, f = D_MODEL, D_FF
    na, ns = N_ATTN_LAYERS, N_SSM_LAYERS
    g, p, gc = SSM_GROUPS, SSM_STATE, SSM_GROUP_CH
    nrm = lambda k, shape, s: jax.random.normal(k, shape, f32) * s
    a_im_base = jnp.pi * jnp.arange(p, dtype=f32)
    return {
        "x": nrm(ks[0], (BATCH, SEQ, d), 1.0),
        "c": nrm(ks[1], (BATCH, d), 1.0),
        "ctx": nrm(ks[2], (BATCH, CTX_LEN, d), 1.0),
        "c_ctx": nrm(ks[3], (d,), 1.0),
        "ada_w": nrm(ks[4], (DEPTH, d, N_MOD * d), 0.5 * d ** -0.5),
        "ada_b": nrm(ks[5], (DEPTH, N_MOD * d), 0.02),
        "norm_g": 1.0 + nrm(ks[6], (DEPTH, N_NORMS, d), 0.02),
        "ffn_w_in": nrm(ks[7], (DEPTH, 2, d, 2 * f), d ** -0.5),
        "ffn_w_out": nrm(ks[8], (DEPTH, 2, f, d), f ** -0.5),
        "attn_w_in": nrm(ks[9], (na, d, QKV_DIM), d ** -0.5),
        "attn_w_out": nrm(ks[10], (na, d, d), d ** -0.5),
        "attn_sink": nrm(ks[11], (na, N_HEADS), 0.5),
        "ssm_w_in": nrm(ks[12], (ns, d, d), d ** -0.5),
        "ssm_a_re": -0.5 + nrm(ks[13], (ns, 2, g, p), 0.01),
        "ssm_a_im": a_im_base + nrm(ks[14], (ns, 2, g, p), 0.01),
        "ssm_log_dt": jax.random.uniform(ks[15], (ns, 2, g), f32, math.log(DT_MIN), math.log(DT_MAX)),
        "ssm_b_re": nrm(ks[16], (ns, 2, g, p, gc), (2 * gc) ** -0.5),
        "ssm_b_im": nrm(ks[17], (ns, 2, g, p, gc), (2 * gc) ** -0.5),
        "ssm_c_re": nrm(ks[18], (ns, 2, g, gc, p), (2 * p) ** -0.5),
        "ssm_c_im": nrm(ks[19], (ns, 2, g, gc, p), (2 * p) ** -0.5),
        "ssm_d": nrm(ks[20], (ns, d), 1.0),
        "ssm_w_glu": nrm(ks[21], (ns, d, 2 * d), d ** -0.5),
    }


def reference(x, c, ctx, c_ctx, ada_w, ada_b, norm_g, ffn_w_in, ffn_w_out,
              attn_w_in, attn_w_out, attn_sink,
              ssm_w_in, ssm_a_re, ssm_a_im, ssm_log_dt, ssm_b_re, ssm_b_im,
              ssm_c_re, ssm_c_im, ssm_d, ssm_w_glu):
    seq_len = x.shape[1]
    ROWS = seq_len // GRID_W
    cos, sin = axial_rope_tables(ROWS)
    c_act = jax.nn.silu(c)
    cc_act = jax.nn.silu(c_ctx)
    for i in range(DEPTH):
        need_ctx = i < DEPTH - 1
        mx = jnp.split((c_act @ ada_w[i] + ada_b[i])[:, None, :], N_MOD, axis=-1)
        mc = jnp.split((cc_act @ ada_w[i] + ada_b[i])[None, None, :], N_MOD, axis=-1)
        g = norm_g[i]
        x = post_sublayer(x, swiglu(pre_sublayer(x, g[0], mx[0], mx[1]), ffn_w_in[i, 0], ffn_w_out[i, 0]), g[1], mx[2], 0.5)
        ctx = post_sublayer(ctx, swiglu(pre_sublayer(ctx, g[0], mc[0], mc[1]), ffn_w_in[i, 0], ffn_w_out[i, 0]), g[1], mc[2], 0.5)
        hx = pre_sublayer(x, g[2], mx[3], mx[4])
        hc = pre_sublayer(ctx, g[2], mc[3], mc[4])
        j = i // N_MIXERS
        if i % N_MIXERS == 0:
            ox, oc = windowed_gqa(hx, hc, attn_w_in[j], attn_w_out[j], attn_sink[j], cos, sin, need_ctx)
        else:
            ox, oc = s5_mixer(hx, hc, ssm_w_in[j], ssm_a_re[j], ssm_a_im[j], ssm_log_dt[j],
                              ssm_b_re[j], ssm_b_im[j], ssm_c_re[j], ssm_c_im[j], ssm_d[j], ssm_w_glu[j], need_ctx)
        x = post_sublayer(x, ox, g[3], mx[5], 1.0)
        x = post_sublayer(x, swiglu(pre_sublayer(x, g[4], mx[6], mx[7]), ffn_w_in[i, 1], ffn_w_out[i, 1]), g[5], mx[8], 0.5)
        if need_ctx:
            ctx = post_sublayer(ctx, oc, g[3], mc[5], 1.0)
            ctx = post_sublayer(ctx, swiglu(pre_sublayer(ctx, g[4], mc[6], mc[7]), ffn_w_in[i, 1], ffn_w_out[i, 1]), g[5], mc[8], 0.5)
    return x
```

```python
import contextlib
import numpy as np
import concourse.bass as bass
import concourse.mybir as mybir
from concourse.bass_utils import run_bass_kernel_spmd

F32 = mybir.dt.float32
BF16 = mybir.dt.bfloat16
AF = mybir.ActivationFunctionType
ALU = mybir.AluOpType


class Ev:
    __slots__ = ("sem", "val", "key")

    def __init__(self, sem, val, key):
        self.sem = sem
        self.val = val
        self.key = key


class Dep:
    __slots__ = ("w", "r", "name")

    def __init__(self, name=""):
        self.w = None
        self.r = {}
        self.name = name


def deps(n, name=""):
    return [Dep(f"{name}{i}") for i in range(n)]


class Eng:
    def __init__(self, name, h, sem):
        self.name = name
        self.h = h
        self.sem = sem
        self.cnt = 0
        self.waited = {}


class Slot:
    __slots__ = ("sem", "val", "key")

    def __init__(self, sem, key):
        self.sem = sem
        self.val = 0
        self.key = key


class Prog:
    NSLOT = 12

    def __init__(self, nc, es):
        self.nc = nc
        self.es = es
        mk = lambda n, h: Eng(n, h, es.enter_context(nc.semaphore("sem_" + n)))
        self.pe = mk("pe", nc.tensor)
        self.act = mk("act", nc.scalar)
        self.dve = mk("dve", nc.vector)
        self.pool = mk("pool", nc.gpsimd)
        self.sp = mk("sp", nc.sync)
        self.engs = [self.pe, self.act, self.dve, self.pool, self.sp]
        self.slots = {}
        self.scur = {}
        for q in (self.sp, self.pool, self.act):
            self.slots[q.name] = [
                Slot(es.enter_context(nc.semaphore(f"dq_{q.name}{i}")), f"dq_{q.name}{i}")
                for i in range(self.NSLOT)
            ]
            self.scur[q.name] = 0
        self.out_evs = []
        self.pool_out = []
        self.n_inst = 0

    def wait(self, eng, ev):
        if ev is None:
            return
        if eng.name == "pe" and ev.key == "e_pe":
            return
        if eng.waited.get(ev.key, 0) >= ev.val:
            return
        eng.h.wait_ge(ev.sem, ev.val)
        eng.waited[ev.key] = ev.val
        self.n_inst += 1

    def sync_for(self, eng, reads, writes):
        for d in reads:
            self.wait(eng, d.w)
        for d in writes:
            self.wait(eng, d.w)
            for ev in d.r.values():
                self.wait(eng, ev)

    def commit(self, ev, reads, writes):
        for d in reads:
            old = d.r.get(ev.key)
            if old is None or old.val < ev.val:
                d.r[ev.key] = ev
        for d in writes:
            d.w = ev
            d.r = {}

    def op(self, eng, fn, reads=(), writes=()):
        self.sync_for(eng, reads, writes)
        inst = fn(eng.h)
        eng.cnt += 1
        inst.then_inc(eng.sem, 1)
        ev = Ev(eng.sem, eng.cnt, "e_" + eng.name)
        self.commit(ev, reads, writes)
        self.n_inst += 1
        return ev

    def mm_group(self, out, pairs, reads, writes, **kw):
        pe = self.pe
        self.sync_for(pe, reads, writes)
        n = len(pairs)
        inst = None
        for i, (l, r) in enumerate(pairs):
            inst = pe.h.matmul(out, l, r, start=(i == 0), stop=(i == n - 1), **kw)
        pe.cnt += 1
        inst.then_inc(pe.sem, 1)
        ev = Ev(pe.sem, pe.cnt, "e_pe")
        self.commit(ev, reads, writes)
        self.n_inst += n
        return ev

    def dma(self, q, out, in_, reads=(), writes=(), is_output=False, **kw):
        if q.name == "pool":
            rows = 1
            for d in list(out.shape)[:-1]:
                rows *= int(d)
            last_b = int(list(in_.shape)[-1]) * 4
            nd = rows * max(1, -(-last_b // 8192)) // 16 + 2
            while self.pool_out and sum(n for _, n in self.pool_out) + nd > 600:
                ev0, _ = self.pool_out.pop(0)
                self.wait(q, ev0)
        sl = self.slots[q.name]
        slot = sl[self.scur[q.name] % len(sl)]
        self.scur[q.name] += 1
        if slot.val > 0:
            self.wait(q, Ev(slot.sem, slot.val, slot.key))
        self.sync_for(q, reads, writes)
        q.h.dma_start(out=out, in_=in_, **kw).then_inc(slot.sem, 16)
        slot.val += 16
        ev = Ev(slot.sem, slot.val, slot.key)
        self.commit(ev, reads, writes)
        if is_output:
            self.out_evs.append(ev)
        if q.name == "pool":
            self.pool_out.append((ev, nd))
        self.n_inst += 1
        return ev

    def barrier(self):
        evs = [Ev(e.sem, e.cnt, "e_" + e.name) for e in self.engs if e.cnt > 0]
        for sl in self.slots.values():
            for s in sl:
                if s.val > 0:
                    evs.append(Ev(s.sem, s.val, s.key))
        for e in self.engs:
            for ev in evs:
                self.wait(e, ev)

    def finish(self):
        evs = []
        for sl in self.slots.values():
            for s in sl:
                if s.val > 0:
                    evs.append(Ev(s.sem, s.val, s.key))
        for e in self.engs:
            if e.cnt > 0:
                evs.append(Ev(e.sem, e.cnt, "e_" + e.name))
        for ev in evs:
            self.wait(self.sp, ev)


D = 2048
F = 5632
KC = D // 128
FC = F // 128
TT = 512
EPS = 1e-6
N_MOD = 9
HEAD_DIM = 128
N_HEADS = 16
N_KV = 4
GRID_W = 64
CTX = 256


def tile_w(W):
    K, N = W.shape
    return np.ascontiguousarray(
        W.reshape(K // 128, 128, N // 128, 128).transpose(2, 1, 0, 3)
    ).reshape(N // 128, 128, (K // 128) * 128)


def fm(v):
    v = np.asarray(v)
    lead = v.shape[:-1]
    n = v.shape[-1] // 128
    a = v.reshape(lead + (n, 128))
    a = np.moveaxis(a, -1, 0)
    return np.ascontiguousarray(a)


class MK:
    def __init__(self, tpc, phases=None, dbg=None, skip_ffn=False):
        self.skip_ffn = skip_ffn
        import os
        self.trunc = int(os.environ.get("MK_TRUNC", "99"))
        self.tpc = tpc
        self.nt_own = tpc // TT
        self.special = "nospecial" not in (phases or [])
        self.ntile = self.nt_own + (1 if self.special else 0)
        self.ntok = tpc + (TT if self.special else 0)
        self.phases = phases
        self.dbg = dbg or []

    def build(self):
        nc = bass.Bass("TRN2", target_bir_lowering=False)
        self.nc = nc
        self.es = contextlib.ExitStack()
        es = self.es
        P = Prog(nc, es)
        self.P = P
        self.declare_io()
        self.alloc_common()
        self.ph_consts()
        self.ph_convert()
        if self.want("modin"):
            self.ph_modin()
        elif self.want("modc"):
            self.ph_mod_compute()
        elif self.want("modd"):
            self.ph_mod_derive()
        self.run_phases()
        P.finish()
        es.close()
        return nc

    def din(self, name, shape, dt=F32):
        return self.nc.dram_tensor(name, list(shape), dt, kind="ExternalInput").ap()

    def dscr(self, name, shape, dt=F32):
        return self.nc.dram_tensor(name, list(shape), dt).ap()

    def sb(self, name, shape, dt=F32, es=None):
        self._uid = getattr(self, "_uid", 0) + 1
        return (es or self.es).enter_context(self.nc.sbuf_tensor(f"{name}_{self._uid}", list(shape), dt))

    def declare_io(self):
        ntok = self.ntok
        if not self.want("modc"):
            self.x_in = self.din("xin", [D, ntok])
        self.fwin_in = [[None] * 2 for _ in range(2)]
        self.fwout_in = [[None] * 2 for _ in range(2)]
        self.fwin_b = [[None] * 2 for _ in range(2)]
        self.fwout_b = [[None] * 2 for _ in range(2)]
        for l in range(2):
            for s in range(2):
                if self.want(f"ffn{l}{s}"):
                    self.fwin_in[l][s] = self.din(f"fwin{l}{s}", [2 * FC, 128, D])
                    self.fwout_in[l][s] = self.din(f"fwout{l}{s}", [KC, 128, F])
                    self.fwin_b[l][s] = self.dscr(f"fwinb{l}{s}", [2 * FC, 128, D], BF16)
                    self.fwout_b[l][s] = self.dscr(f"fwoutb{l}{s}", [KC, 128, F], BF16)
        self.fwin_dep = [[Dep() for s in range(2)] for l in range(2)]
        self.fwout_dep = [[Dep() for s in range(2)] for l in range(2)]
        self.xs = self.dscr("xs", [D, ntok])
        self.xs_dep = deps(self.ntile, "xs")
        if self.want("yout"):
            self.y_out = self.nc.dram_tensor("yout", [D, self.tpc], F32, kind="ExternalOutput").ap()
        if self.want("ssmu"):
            self.suw_in = self.din("suw", [KC, 128, D])
            self.suw_b = self.dscr("suwb", [KC, 128, D], BF16)
            self.suw_dep = Dep()
            self.x4_out = self.nc.dram_tensor("x4", [D, ntok], F32, kind="ExternalOutput").ap()
            self.u_out = self.nc.dram_tensor("uTo", [D, ntok], F32, kind="ExternalOutput").ap()
        if self.want("glu"):
            self.gw_in = self.din("gw", [2 * KC, 128, D])
            self.gw_b = self.dscr("gwb", [2 * KC, 128, D], BF16)
            self.gw_dep = Dep()
            self.g_in = self.din("gin", [D, ntok])
        if self.want("attn"):
            self.attn_decl()
        if self.want("modin"):
            self.modin = [self.din(n, [128, 6, KC, 2]) for n in ("modA", "modB", "modG")]
        self.dbg_out = {}
        for name, shape in self.dbg:
            self.dbg_out[name] = self.nc.dram_tensor("dbg_" + name, list(shape), F32, kind="ExternalOutput").ap()

    def alloc_common(self):
        nc = self.nc
        es = self.es
        self.psum = [es.enter_context(nc.psum_tensor(f"ps{i}", [128, TT], F32)) for i in range(8)]
        self.psd = deps(8, "ps")
        self.ones = self.sb("ones", [128, 128], BF16)
        self.ones_d = Dep("ones")
        self.epsc = self.sb("epsc", [128, 1])
        self.ones32 = self.sb("ones32", [128, 128], F32)
        self.modA = self.sb("modA_sb", [128, 6, KC, 2])
        self.modB = self.sb("modB_sb", [128, 6, KC, 2])
        self.modG = self.sb("modG_sb", [128, 6, KC, 2])
        self.mod_d = Dep("mod")

    def ph_consts(self):
        P = self.P
        P.op(P.dve, lambda h: h.memset(self.ones[:], 1.0), writes=[self.ones_d])
        P.op(P.dve, lambda h: h.memset(self.epsc[:], EPS), writes=[self.ones_d])
        P.op(P.dve, lambda h: h.memset(self.ones32[:], 1.0), writes=[self.ones_d])

    def convert(self, src, dst, dep, grp=8):
        P = self.P
        n = src.shape[0]
        for j0 in range(0, n, grp):
            j1 = min(n, j0 + grp)
            if j1 - j0 == 1:
                P.dma(P.pool, dst[j0], src[j0], writes=[dep], max_dma_last_dim=8192)
            else:
                P.dma(P.pool, dst[j0:j1].rearrange("j p e -> p j e"), src[j0:j1].rearrange("j p e -> p j e"),
                      writes=[dep], max_dma_last_dim=8192)

    def ph_convert(self):
        if self.want("attn"):
            self.attn_convert()
        if self.want("ssmu"):
            self.convert(self.suw_in, self.suw_b, self.suw_dep)
        if self.want("glu"):
            self.convert(self.gw_in, self.gw_b, self.gw_dep)
        for l in range(2):
            for s in range(2):
                if self.want(f"ffn{l}{s}"):
                    self.convert(self.fwin_in[l][s], self.fwin_b[l][s], self.fwin_dep[l][s])
                    self.convert(self.fwout_in[l][s], self.fwout_b[l][s], self.fwout_dep[l][s], grp=1)

    def ph_modin(self):
        P = self.P
        for src, dst in zip(self.modin, (self.modA, self.modB, self.modG)):
            P.dma(P.sp, dst[:], src, writes=[self.mod_d])

    def want(self, ph):
        return ph in self.phases

    def ph_mod_compute(self):
        P = self.P
        NJ = N_MOD * KC // 8
        cm_in = self.din("cm3", [128, KC, 3])
        ada_in = [self.din(f"ada{l}", [NJ, 128, D]) for l in range(2)]
        adab_in = self.din("adab", [128, 2, NJ])
        out = self.nc.dram_tensor("modraw", [128, 2, NJ, 3], F32, kind="ExternalOutput").ap()
        with contextlib.ExitStack() as es:
            cm = self.sb("cm_sb", [128, KC, 3], es=es)
            cs = self.sb("cs_sb", [128, KC, 3], es=es)
            adab = self.sb("adab_sb", [128, 2, NJ], es=es)
            modr = self.sb("modr", [128, 2, NJ, 3], es=es)
            wb = [self.sb(f"adaw{i}", [128, 2, D], es=es) for i in range(2)]
            wbd = deps(2, "adaw")
            d_cm, d_cs, d_adab, d_modr = Dep(), Dep(), Dep(), Dep()
            P.dma(P.sp, cm[:], cm_in, writes=[d_cm])
            P.dma(P.sp, adab[:], adab_in, writes=[d_adab])
            P.op(P.act, lambda h: h.activation(out=cs[:], in_=cm[:], func=AF.Silu), reads=[d_cm], writes=[d_cs])
            it = 0
            for l in range(2):
                for j0 in range(0, NJ, 2):
                    b = it % 2
                    it += 1
                    P.dma(P.sp, wb[b][:], ada_in[l][j0:j0 + 2].rearrange("j p e -> p j e"), writes=[wbd[b]])
                    for jj in range(2):
                        j = j0 + jj
                        pb = j % 2
                        ps = self.psum[pb]
                        P.mm_group(ps[:, 0:3],
                                   [(wb[b][:, jj, k * 128:(k + 1) * 128], cs[:, k, :]) for k in range(KC)],
                                   reads=[wbd[b], d_cs], writes=[self.psd[pb]])
                        P.op(P.dve, lambda h, ps=ps, l=l, j=j: h.tensor_tensor(
                            out=modr[:, l, j, :], in0=ps[:, 0:3], in1=adab[:, l, j:j + 1].to_broadcast([128, 3]),
                            op=ALU.add), reads=[self.psd[pb], d_adab], writes=[d_modr])
            P.dma(P.sp, out, modr[:], reads=[d_modr], writes=[Dep()], is_output=True)
            P.barrier()

    def ph_mod_derive(self):
        P = self.P
        modr_in = self.din("modr2", [128, 2, N_MOD * KC, 2])
        ng_in = self.din("ng", [128, 12, KC])
        with contextlib.ExitStack() as es:
            modr = self.sb("modr", [128, 2, N_MOD * KC, 2], es=es)
            ng = self.sb("ng_sb", [128, 12, KC], es=es)
            d_modr, d_ng = Dep(), Dep()
            P.dma(P.sp, modr[:], modr_in, writes=[d_modr])
            P.dma(P.sp, ng[:], ng_in, writes=[d_ng])
            for l in range(2):
                for s3 in range(3):
                    idx = l * 3 + s3
                    sh = modr[:, l, (3 * s3) * KC:(3 * s3 + 1) * KC, :]
                    sc = modr[:, l, (3 * s3 + 1) * KC:(3 * s3 + 2) * KC, :]
                    gt = modr[:, l, (3 * s3 + 2) * KC:(3 * s3 + 3) * KC, :]
                    gpre = ng[:, l * 6 + 2 * s3, :]
                    gpost = ng[:, l * 6 + 2 * s3 + 1, :]
                    wgt = 1.0 if s3 == 1 else 0.5
                    for r in range(2):
                        P.op(P.dve, lambda h, sc=sc, gpre=gpre, idx=idx, r=r: h.scalar_tensor_tensor(
                            out=self.modA[:, idx, :, r], in0=sc[:, :, r], scalar=1.0, in1=gpre,
                            op0=ALU.add, op1=ALU.mult), reads=[d_modr, d_ng], writes=[self.mod_d])
                        P.op(P.dve, lambda h, sh=sh, idx=idx, r=r: h.tensor_copy(
                            out=self.modB[:, idx, :, r], in_=sh[:, :, r]), reads=[d_modr], writes=[self.mod_d])
                        P.op(P.dve, lambda h, gt=gt, gpost=gpost, idx=idx, r=r, wgt=wgt: h.scalar_tensor_tensor(
                            out=self.modG[:, idx, :, r], in0=gt[:, :, r], scalar=wgt, in1=gpost,
                            op0=ALU.mult, op1=ALU.mult), reads=[d_modr, d_ng], writes=[self.mod_d])
            P.barrier()

    def tile_cols(self, t):
        return t * TT

    def segs(self, t):
        if t < self.nt_own:
            return [(0, TT, 0)]
        return [(0, TT // 2, 0), (TT // 2, TT, 1)]

    def sumsq_acc(self, B, c, src_ap, src_dep):
        P = self.P
        i = c % 2
        P.op(P.act, lambda h: h.activation(out=B["sqf"][:, i, :], in_=src_ap, func=AF.Square),
             reads=[src_dep], writes=[B["sqfd"][i]])
        if c == 0:
            P.op(P.dve, lambda h: h.tensor_copy(out=B["acc"][:], in_=B["sqf"][:, i, :]),
                 reads=[B["sqfd"][i]], writes=[B["accd"]])
        else:
            P.op(P.dve, lambda h: h.tensor_tensor(out=B["acc"][:], in0=B["acc"][:], in1=B["sqf"][:, i, :], op=ALU.add),
                 reads=[B["sqfd"][i], B["accd"]], writes=[B["accd"]])

    def sumsq_finish(self, B, bank, rstd, rstd_d):
        P = self.P
        P.op(P.pe, lambda h: h.matmul(self.psum[bank][:], self.ones32[:], B["acc"][:], start=True, stop=True),
             reads=[B["accd"], self.ones_d], writes=[self.psd[bank]])
        self.rstd_from_bank(B, bank, rstd, rstd_d)

    def rms_rstd(self, B, src_chunks, src_deps, rstd, rstd_d):
        for c in range(KC):
            self.sumsq_acc(B, c, src_chunks[c], src_deps[c])
        self.sumsq_finish(B, 6, rstd, rstd_d)

    def rstd_from_bank(self, B, bank, rstd, rstd_d):
        P = self.P
        tmp, tmpd = B["rt"], B["rtd"]
        P.op(P.act, lambda h: h.activation(out=tmp[:], in_=self.psum[bank][:], func=AF.Sqrt,
                                           bias=self.epsc[:], scale=1.0 / D),
             reads=[self.psd[bank], self.ones_d], writes=[tmpd])
        P.op(P.dve, lambda h: h.reciprocal(out=rstd[:], in_=tmp[:]), reads=[tmpd], writes=[rstd_d])

    def ffn_bufs(self, es):
        B = {}
        B["x"] = self.sb("f_x", [128, KC, TT], F32, es)
        B["xd"] = deps(KC, "x")
        B["o"] = self.sb("f_o", [128, KC, TT], BF16, es)
        B["od"] = deps(KC, "o")
        B["hx"] = self.sb("f_hx", [128, KC, TT], BF16, es)
        B["hxd"] = deps(KC, "hx")
        B["sqf"] = self.sb("f_sqf", [128, 2, TT], F32, es)
        B["sqfd"] = deps(2, "sqf")
        B["acc"] = self.sb("f_acc", [128, TT], F32, es)
        B["accd"] = Dep("acc")
        B["hid"] = self.sb("f_hid", [128, FC, TT], BF16, es)
        B["hidd"] = deps(FC, "hid")
        B["ts"] = self.sb("f_ts", [128, 2, TT], BF16, es)
        B["tsd"] = deps(2, "ts")
        B["xn"] = self.sb("f_xn", [128, 2, TT], F32, es)
        B["xnd"] = deps(2, "xn")
        B["rt"] = self.sb("f_rt", [128, TT], F32, es)
        B["rtd"] = Dep("rt")
        B["rstd"] = self.sb("f_rstd", [128, TT], F32, es)
        B["rstdd"] = Dep("rstd")
        B["rstd2"] = self.sb("f_rstd2", [128, TT], F32, es)
        B["rstd2d"] = Dep("rstd2")
        NWB = 4
        B["win"] = self.sb("f_win", [128, NWB, D], BF16, es)
        B["wind"] = deps(NWB, "win")
        B["wout"] = self.sb("f_wout", [128, 2, F], BF16, es)
        B["woutd"] = deps(2, "wout")
        return B

    def load_x(self, B, t, src=None, src_dep=None):
        P = self.P
        c0 = self.tile_cols(t)
        src = self.xs if src is None else src
        sd = self.xs_dep[t] if src_dep is None else src_dep
        for h in range(2):
            ks = slice(h * 8, h * 8 + 8)
            P.dma(P.sp if h == 0 else P.act, B["x"][:, ks, :],
                  src[h * 1024:(h + 1) * 1024, c0:c0 + TT].rearrange("(k p) n -> p k n", p=128),
                  reads=[sd], writes=B["xd"][h * 8:h * 8 + 8])

    def store_x(self, B, t, dst=None, dst_dep=None, is_output=False, ncols=TT):
        P = self.P
        c0 = self.tile_cols(t)
        dst = self.xs if dst is None else dst
        dd = self.xs_dep[t] if dst_dep is None else dst_dep
        for h in range(2):
            ks = slice(h * 8, h * 8 + 8)
            P.dma(P.sp, dst[h * 1024:(h + 1) * 1024, c0:c0 + ncols].rearrange("(k p) n -> p k n", p=128),
                  B["x"][:, ks, 0:ncols], reads=B["xd"][h * 8:h * 8 + 8], writes=[dd], is_output=is_output)

    def pre_norm(self, B, t, idx):
        P = self.P
        x, xd = B["x"], B["xd"]
        self.rms_rstd(B, [x[:, c, :] for c in range(KC)], xd, B["rstd"], B["rstdd"])
        for c in range(KC):
            i = c % 2
            P.op(P.dve, lambda h, c=c, i=i: h.tensor_tensor(out=B["xn"][:, i, :], in0=x[:, c, :], in1=B["rstd"][:],
                                                           op=ALU.mult),
                 reads=[xd[c], B["rstdd"]], writes=[B["xnd"][i]])
            for (a, b, r) in self.segs(t):
                P.op(P.act, lambda h, c=c, i=i, a=a, b=b, r=r: h.activation(
                    out=B["hx"][:, c, a:b], in_=B["xn"][:, i, a:b], func=AF.Identity,
                    bias=self.modB[:, idx, c, r:r + 1], scale=self.modA[:, idx, c, r:r + 1]),
                     reads=[B["xnd"][i], self.mod_d], writes=[B["hxd"][c]])

    def post_update(self, B, t, idx):
        P = self.P
        x, xd = B["x"], B["xd"]
        for c in range(KC):
            i = c % 2
            P.op(P.dve, lambda h, c=c, i=i: h.tensor_tensor(out=B["xn"][:, i, :], in0=B["o"][:, c, :],
                                                           in1=B["rstd2"][:], op=ALU.mult),
                 reads=[B["od"][c], B["rstd2d"]], writes=[B["xnd"][i]])
            for (a, b, r) in self.segs(t):
                P.op(P.dve, lambda h, c=c, i=i, a=a, b=b, r=r: h.scalar_tensor_tensor(
                    out=x[:, c, a:b], in0=B["xn"][:, i, a:b], scalar=self.modG[:, idx, c, r:r + 1],
                    in1=x[:, c, a:b], op0=ALU.mult, op1=ALU.add),
                     reads=[B["xnd"][i], self.mod_d, xd[c]], writes=[xd[c]])

    def ffn_core(self, B, l, s):
        P = self.P
        win_b, wout_b = self.fwin_b[l][s], self.fwout_b[l][s]
        NWB = len(B["wind"])
        wi = 0
        for j in range(FC):
            bi = []
            for which in range(2):
                b = wi % NWB
                wi += 1
                jj = j + which * FC
                P.dma(P.sp, B["win"][:, b, :], win_b[jj], reads=[self.fwin_dep[l][s]], writes=[B["wind"][b]])
                bi.append(b)
            pg, pu = (2 * (j % 3)), (2 * (j % 3) + 1)
            for which, pb in ((0, pg), (1, pu)):
                b = bi[which]
                P.mm_group(self.psum[pb][:],
                           [(B["win"][:, b, k * 128:(k + 1) * 128], B["hx"][:, k, :]) for k in range(KC)],
                           reads=[B["wind"][b]] + B["hxd"], writes=[self.psd[pb]])
            ti = j % 2
            P.op(P.act, lambda h, pg=pg, ti=ti: h.activation(out=B["ts"][:, ti, :], in_=self.psum[pg][:], func=AF.Silu),
                 reads=[self.psd[pg]], writes=[B["tsd"][ti]])
            P.op(P.dve, lambda h, pu=pu, ti=ti, j=j: h.tensor_tensor(out=B["hid"][:, j, :], in0=B["ts"][:, ti, :],
                                                                    in1=self.psum[pu][:], op=ALU.mult),
                 reads=[B["tsd"][ti], self.psd[pu]], writes=[B["hidd"][j]])
        for c in range(KC):
            b = c % 2
            P.dma(P.sp, B["wout"][:, b, :], wout_b[c], reads=[self.fwout_dep[l][s]], writes=[B["woutd"][b]])
            pb = c % 6
            P.mm_group(self.psum[pb][:],
                       [(B["wout"][:, b, j * 128:(j + 1) * 128], B["hid"][:, j, :]) for j in range(FC)],
                       reads=[B["woutd"][b]] + B["hidd"], writes=[self.psd[pb]])
            self.sumsq_acc(B, c, self.psum[pb][:], self.psd[pb])
            P.op(P.dve, lambda h, pb=pb, c=c: h.tensor_copy(out=B["o"][:, c, :], in_=self.psum[pb][:]),
                 reads=[self.psd[pb]], writes=[B["od"][c]])
        self.sumsq_finish(B, 7, B["rstd2"], B["rstd2d"])

    def ffn_sublayer(self, B, t, l, s):
        idx = l * 3 + (0 if s == 0 else 2)
        self.pre_norm(B, t, idx)
        self.ffn_core(B, l, s)
        self.post_update(B, t, idx)

    def sweep_ffn(self, l, s, tiles, src=None, final=False):
        P = self.P
        with contextlib.ExitStack() as es:
            B = self.ffn_bufs(es)
            for t in tiles:
                if src is not None:
                    self.load_x(B, t, src=src, src_dep=Dep())
                else:
                    self.load_x(B, t)
                self.ffn_sublayer(B, t, l, s)
                if final:
                    self.store_x(B, t, dst=self.y_out, dst_dep=Dep(), is_output=True)
                else:
                    self.store_x(B, t)
            P.barrier()

    def sweep_ffn_u(self, tiles):
        P = self.P
        l = 1
        with contextlib.ExitStack() as es:
            B = self.ffn_bufs(es)
            ub = self.sb("u_buf", [128, 2, TT], F32, es)
            ubd = deps(2)
            for t in tiles:
                c0 = self.tile_cols(t)
                self.load_x(B, t)
                self.ffn_sublayer(B, t, l, 0)
                self.store_x(B, t, dst=self.x4_out, dst_dep=Dep(), is_output=True)
                self.pre_norm(B, t, l * 3 + 1)
                for c in range(KC):
                    b = c % 4
                    P.dma(P.sp, B["win"][:, b, :], self.suw_b[c], reads=[self.suw_dep], writes=[B["wind"][b]])
                    pb = c % 4
                    P.mm_group(self.psum[pb][:],
                               [(B["win"][:, b, k * 128:(k + 1) * 128], B["hx"][:, k, :]) for k in range(KC)],
                               reads=[B["wind"][b]] + B["hxd"], writes=[self.psd[pb]])
                    i = c % 2
                    P.op(P.act, lambda h, pb=pb, i=i: h.activation(out=ub[:, i, :], in_=self.psum[pb][:], func=AF.Copy),
                         reads=[self.psd[pb]], writes=[ubd[i]])
                    P.dma(P.sp, self.u_out[c * 128:(c + 1) * 128, c0:c0 + TT], ub[:, i, :], reads=[ubd[i]], writes=[Dep()],
                          is_output=True)
            P.barrier()

    def sweep_glu_ffn(self, tiles):
        P = self.P
        l = 1
        with contextlib.ExitStack() as es:
            B = self.ffn_bufs(es)
            gf = self.sb("g_f", [128, 2, TT], F32, es)
            gfd = deps(2)
            for t in tiles:
                c0 = self.tile_cols(t)
                self.load_x(B, t, src=self.x_in, src_dep=Dep())
                for c in range(KC):
                    i = c % 2
                    P.dma(P.act, gf[:, i, :], self.g_in[c * 128:(c + 1) * 128, c0:c0 + TT], writes=[gfd[i]])
                    P.op(P.act, lambda h, c=c, i=i: h.activation(out=B["hx"][:, c, :], in_=gf[:, i, :], func=AF.Copy),
                         reads=[gfd[i]], writes=[B["hxd"][c]])
                for c in range(KC):
                    bs = []
                    for which in range(2):
                        b = (2 * c + which) % 4
                        P.dma(P.sp, B["win"][:, b, :], self.gw_b[c + which * KC], reads=[self.gw_dep], writes=[B["wind"][b]])
                        bs.append(b)
                    pv, pg = 2 * (c % 3), 2 * (c % 3) + 1
                    for which, pb in ((0, pv), (1, pg)):
                        b = bs[which]
                        P.mm_group(self.psum[pb][:],
                                   [(B["win"][:, b, k * 128:(k + 1) * 128], B["hx"][:, k, :]) for k in range(KC)],
                                   reads=[B["wind"][b]] + B["hxd"], writes=[self.psd[pb]])
                    ti = c % 2
                    P.op(P.act, lambda h, pg=pg, ti=ti: h.activation(out=B["xn"][:, ti, :], in_=self.psum[pg][:], func=AF.Sigmoid),
                         reads=[self.psd[pg]], writes=[B["xnd"][ti]])
                    P.op(P.dve, lambda h, pv=pv, ti=ti: h.tensor_tensor(out=B["xn"][:, ti, :], in0=B["xn"][:, ti, :],
                                                                     in1=self.psum[pv][:], op=ALU.mult),
                         reads=[B["xnd"][ti], self.psd[pv]], writes=[B["xnd"][ti]])
                    self.sumsq_acc(B, c, B["xn"][:, ti, :], B["xnd"][ti])
                    P.op(P.dve, lambda h, c=c, ti=ti: h.tensor_copy(out=B["o"][:, c, :], in_=B["xn"][:, ti, :]),
                         reads=[B["xnd"][ti]], writes=[B["od"][c]])
                self.sumsq_finish(B, 7, B["rstd2"], B["rstd2d"])
                self.post_update(B, t, l * 3 + 1)
                self.ffn_sublayer(B, t, l, 1)
                self.store_x(B, t, dst=self.y_out, dst_dep=Dep(), is_output=True)
            P.barrier()

    def attn_decl(self):
        nt = self.ntok
        self.aqw_in = self.din("aqw", [32, 128, D])
        self.akw_in = self.din("akw", [8, 128, D])
        self.avw_in = self.din("avw", [128, KC * 512])
        self.aow_in = self.din("aow", [KC, 128, D])
        self.aqw_b = self.dscr("aqwb", [32, 128, D], BF16)
        self.akw_b = self.dscr("akwb", [8, 128, D], BF16)
        self.avw_b = self.dscr("avwb", [128, KC * 512], BF16)
        self.aow_b = self.dscr("aowb", [KC, 128, D], BF16)
        self.aw_dep = Dep("aw")
        self.rope_in = self.din("rope", [4, 128, nt])
        self.mask_in = self.din("amask", [128, 4, 384])
        self.sink_in = self.din("asink", [128, N_HEADS])
        self.ident_in = self.din("ident", [128, 128])
        self.kT_s = self.dscr("kTs", [N_KV, 128, nt], BF16)
        self.v_s = self.dscr("vs", [nt // 128, 128, 512], BF16)
        self.kv_dep = deps(self.ntile, "kv")

    def attn_convert(self):
        self.convert(self.aqw_in, self.aqw_b, self.aw_dep)
        self.convert(self.akw_in, self.akw_b, self.aw_dep)
        P = self.P
        for k0 in range(0, KC, 4):
            P.dma(P.pool, self.avw_b[:, k0 * 512:(k0 + 4) * 512], self.avw_in[:, k0 * 512:(k0 + 4) * 512],
                  writes=[self.aw_dep], max_dma_last_dim=8192)
        self.convert(self.aow_in, self.aow_b, self.aw_dep)

    def kv_bufs(self, es, B):
        A = {}
        A["cs"] = self.sb("a_cs", [128, 2, TT], F32, es)
        A["csd"] = Dep("cs")
        A["kb"] = self.sb("a_kb", [128, N_KV, TT], BF16, es)
        A["kbd"] = Dep("kb")
        A["vb"] = self.sb("a_vb", [128, 4, 512], BF16, es)
        A["vbd"] = Dep("vb")
        A["t1"] = B["xn"]
        A["t1d"] = B["xnd"]
        A["wk"] = B["win"]
        A["wkd"] = B["wind"]
        A["wv"] = B["wout"][:].rearrange("p a f -> p (a f)")[:, 0:KC * 512]
        A["wvd"] = list(B["woutd"])
        return A

    def rope_evac(self, A, pa, pb, out_ap, out_dep, cos_ap, sin_ap):
        P = self.P
        i0, i1 = 0, 1
        P.op(P.dve, lambda h: h.tensor_tensor(out=A["t1"][:, i0, :], in0=self.psum[pb][:], in1=sin_ap, op=ALU.mult),
             reads=[self.psd[pb], A["csd"]], writes=[A["t1d"][i0]])
        P.op(P.dve, lambda h: h.tensor_tensor(out=A["t1"][:, i1, :], in0=self.psum[pa][:], in1=cos_ap, op=ALU.mult),
             reads=[self.psd[pa], A["csd"]], writes=[A["t1d"][i1]])
        P.op(P.dve, lambda h: h.tensor_tensor(out=out_ap, in0=A["t1"][:, i0, :], in1=A["t1"][:, i1, :], op=ALU.add),
             reads=A["t1d"], writes=[out_dep])

    def kv_project(self, B, A, t):
        P = self.P
        c0 = self.tile_cols(t)
        P.dma(P.act, A["cs"][:], self.rope_in[0:2, :, c0:c0 + TT].rearrange("a p n -> p a n"), writes=[A["csd"]])
        for g in range(N_KV):
            bs = []
            for which in range(2):
                b = (2 * g + which) % 4
                P.dma(P.sp, A["wk"][:, b, :], self.akw_b[g + which * N_KV], reads=[self.aw_dep], writes=[A["wkd"][b]])
                bs.append(b)
            pa, pb = 2 * (g % 2), 2 * (g % 2) + 1
            for which, pbk in ((0, pa), (1, pb)):
                b = bs[which]
                P.mm_group(self.psum[pbk][:],
                           [(A["wk"][:, b, k * 128:(k + 1) * 128], B["hx"][:, k, :]) for k in range(KC)],
                           reads=[A["wkd"][b]] + B["hxd"], writes=[self.psd[pbk]])
            self.rope_evac(A, pa, pb, A["kb"][:, g, :], A["kbd"], A["cs"][:, 0, :], A["cs"][:, 1, :])
        P.dma(P.sp, self.kT_s[:, :, c0:c0 + TT].rearrange("g p n -> p g n"), A["kb"][:], reads=[A["kbd"]],
              writes=[self.kv_dep[t]])
        for blk in range(4):
            pbk = 4 + blk % 2
            P.mm_group(self.psum[pbk][:],
                       [(B["hx"][:, k, blk * 128:(blk + 1) * 128], A["wv"][:, k * 512:(k + 1) * 512]) for k in range(KC)],
                       reads=A["wvd"] + B["hxd"], writes=[self.psd[pbk]])
            P.op(P.act, lambda h, pbk=pbk, blk=blk: h.activation(out=A["vb"][:, blk, :], in_=self.psum[pbk][:], func=AF.Copy),
                 reads=[self.psd[pbk]], writes=[A["vbd"]])
        P.dma(P.sp, self.v_s[c0 // 128:c0 // 128 + 4].rearrange("b p n -> p b n"), A["vb"][:], reads=[A["vbd"]],
              writes=[self.kv_dep[t]])

    def sweep_ffn_kv(self, l, s, tiles, src=None):
        P = self.P
        with contextlib.ExitStack() as es:
            B = self.ffn_bufs(es)
            A = self.kv_bufs(es, B)
            for t in tiles:
                if src is not None:
                    self.load_x(B, t, src=src, src_dep=Dep())
                else:
                    self.load_x(B, t)
                if not self.skip_ffn:
                    self.ffn_sublayer(B, t, l, s)
                self.store_x(B, t)
                self.pre_norm(B, t, l * 3 + 1)
                P.dma(P.act, A["wv"], self.avw_b, reads=[self.aw_dep], writes=A["wvd"])
                self.kv_project(B, A, t)
            P.barrier()

    def win_cols(self, t):
        tpc = self.tpc
        out = []
        for i in range(6):
            tb = t * TT - 128 + 128 * i
            if tb < 0:
                src = tpc
            elif tb >= tpc:
                src = tpc + 128
            else:
                src = tb
            out.append((128 * i, src, 128))
        return out

    def attn_bufs(self, es):
        A = {}
        A["cs"] = self.sb("b_cs", [128, 2, TT], F32, es)
        A["csd"] = Dep()
        A["t1"] = self.sb("b_t1", [128, 2, TT], F32, es)
        A["t1d"] = deps(2)
        A["q"] = self.sb("b_q", [128, N_HEADS, TT], BF16, es)
        A["qd"] = deps(N_HEADS)
        A["ao"] = self.sb("b_ao", [128, N_HEADS, TT], BF16, es)
        A["aod"] = deps(N_HEADS)
        A["kw"] = self.sb("b_kw", [128, N_KV, 768], BF16, es)
        A["kwd"] = Dep()
        A["vw"] = self.sb("b_vw", [128, 6, 512], BF16, es)
        A["vwd"] = Dep()
        A["kc"] = self.sb("b_kc", [128, N_KV, CTX], BF16, es)
        A["vc"] = self.sb("b_vc", [128, 2, 512], BF16, es)
        A["kcd"] = Dep()
        A["mask"] = self.sb("b_mask", [128, 4, 384], F32, es)
        A["sink"] = self.sb("b_sink", [128, N_HEADS], F32, es)
        A["cd"] = Dep()
        A["ident"] = self.sb("b_ident", [128, 128], BF16, es)
        A["s"] = self.sb("b_s", [128, 4, 640], F32, es)
        A["sd"] = deps(4)
        A["e"] = self.sb("b_e", [128, 4, 640], BF16, es)
        A["ed"] = deps(4)
        A["en"] = self.sb("b_en", [128, 4, 640], BF16, es)
        A["end"] = deps(4)
        A["pT"] = self.sb("b_pT", [128, 4, 640], BF16, es)
        A["pTd"] = deps(4)
        A["st"] = self.sb("b_st", [128, 4, 8], F32, es)
        A["std"] = deps(4)
        A["wq"] = self.sb("b_wq", [128, 4, D], BF16, es)
        A["wqd"] = deps(4)
        return A

    def attn_tile(self, B, A, t, qblocks):
        P = self.P
        ident = A["ident"]
        units = [(qb, mk, h) for (qb, mk) in qblocks for h in range(N_HEADS)]
        ND = len(A["sd"])

        def st1(u, qb, mk, h):
            g = h // 4
            i = u % ND
            qs = slice(qb * 128, qb * 128 + 128)
            pbA, pbB = 2 + (u % 2), 4 + (u % 2)
            P.mm_group(self.psum[pbA][:, 0:384], [(A["q"][:, h, qs], A["kw"][:, g, qb * 128:qb * 128 + 384])],
                       reads=[A["qd"][h], A["kwd"]], writes=[self.psd[pbA]])
            P.mm_group(self.psum[pbB][:, 0:CTX], [(A["q"][:, h, qs], A["kc"][:, g, :])],
                       reads=[A["qd"][h], A["kcd"]], writes=[self.psd[pbB]])
            S = A["s"][:, i, :]
            P.op(P.dve, lambda hh: hh.tensor_tensor(
                out=S[:, 0:384], in0=self.psum[pbA][:, 0:384], in1=A["mask"][:, mk, :], op=ALU.add),
                 reads=[self.psd[pbA], A["cd"]], writes=[A["sd"][i]])
            P.op(P.act, lambda hh: hh.activation(out=S[:, 384:640], in_=self.psum[pbB][:, 0:CTX], func=AF.Copy),
                 reads=[self.psd[pbB]], writes=[A["sd"][i]])
            st = A["st"][:, i, :]
            P.op(P.dve, lambda hh: hh.reduce_max(out=st[:, 0:1], in_=S, axis=mybir.AxisListType.X),
                 reads=[A["sd"][i]], writes=[A["std"][i]])
            P.op(P.dve, lambda hh: hh.tensor_tensor(out=st[:, 0:1], in0=st[:, 0:1], in1=A["sink"][:, h:h + 1], op=ALU.max),
                 reads=[A["std"][i], A["cd"]], writes=[A["std"][i]])
            P.op(P.dve, lambda hh: hh.tensor_scalar(st[:, 1:2], st[:, 0:1], -1.0, None, ALU.mult),
                 reads=[A["std"][i]], writes=[A["std"][i]])

        def st2(u, qb, mk, h):
            i = u % ND
            S = A["s"][:, i, :]
            st = A["st"][:, i, :]
            E = A["e"][:, i, :]
            P.op(P.act, lambda hh: hh.activation(out=E, in_=S, func=AF.Exp, bias=st[:, 1:2], scale=1.0,
                                                 accum_out=st[:, 2:3]),
                 reads=[A["sd"][i], A["std"][i]], writes=[A["ed"][i], A["std"][i]])
            P.op(P.act, lambda hh: hh.activation(out=st[:, 3:4], in_=A["sink"][:, h:h + 1], func=AF.Exp,
                                                 bias=st[:, 1:2], scale=1.0),
                 reads=[A["std"][i], A["cd"]], writes=[A["std"][i]])
            P.op(P.dve, lambda hh: hh.tensor_tensor(out=st[:, 4:5], in0=st[:, 2:3], in1=st[:, 3:4], op=ALU.add),
                 reads=[A["std"][i]], writes=[A["std"][i]])
            P.op(P.dve, lambda hh: hh.reciprocal(out=st[:, 5:6], in_=st[:, 4:5]),
                 reads=[A["std"][i]], writes=[A["std"][i]])
            EN = A["en"][:, i, :]
            P.op(P.dve, lambda hh: hh.tensor_scalar(EN, E, st[:, 5:6], None, ALU.mult),
                 reads=[A["ed"][i], A["std"][i]], writes=[A["end"][i]])

        def st3(u, qb, mk, h):
            i = u % ND
            EN = A["en"][:, i, :]
            pt_ps = self.psum[6][:].bitcast(BF16)
            P.sync_for(P.pe, [A["end"][i], A["cd"]], [self.psd[6]])
            inst = None
            for blk in range(5):
                inst = P.pe.h.transpose(pt_ps[:, blk * 128:(blk + 1) * 128], EN[:, blk * 128:(blk + 1) * 128], ident[:])
            P.pe.cnt += 1
            inst.then_inc(P.pe.sem, 1)
            ev = Ev(P.pe.sem, P.pe.cnt, "e_pe")
            P.commit(ev, [A["end"][i], A["cd"]], [self.psd[6]])
            PT = A["pT"][:, i, :]
            P.op(P.act, lambda hh: hh.activation(out=PT, in_=pt_ps[:, 0:640], func=AF.Copy),
                 reads=[self.psd[6]], writes=[A["pTd"][i]])

        def st4(u, qb, mk, h):
            g = h // 4
            i = u % ND
            qs = slice(qb * 128, qb * 128 + 128)
            PT = A["pT"][:, i, :]
            pairs = []
            for blk in range(3):
                pairs.append((A["vw"][:, qb + blk, g * 128:(g + 1) * 128], PT[:, blk * 128:(blk + 1) * 128]))
            for blk in range(2):
                pairs.append((A["vc"][:, blk, g * 128:(g + 1) * 128], PT[:, (3 + blk) * 128:(4 + blk) * 128]))
            P.mm_group(self.psum[7][:, 0:128], pairs, reads=[A["vwd"], A["kcd"], A["pTd"][i]], writes=[self.psd[7]])
            P.op(P.act, lambda hh: hh.activation(out=A["ao"][:, h, qs], in_=self.psum[7][:, 0:128], func=AF.Copy),
                 reads=[self.psd[7]], writes=[A["aod"][h]])

        stages = [st1, st2, st3, st4]
        if self.trunc < 99:
            stages = stages[:max(1, min(4, self.trunc - 1))]
        n = len(units)
        for step in range(n + len(stages) - 1):
            for si, fn in enumerate(stages):
                u = step - si
                if 0 <= u < n:
                    fn(u, *units[u])

    def sweep_attn(self, tiles):
        P = self.P
        l = 0
        tpc = self.tpc
        with contextlib.ExitStack() as es:
            B = self.ffn_bufs_small(es)
            A = self.attn_bufs(es)
            P.dma(P.act, A["mask"][:], self.mask_in, writes=[A["cd"]])
            P.dma(P.act, A["sink"][:], self.sink_in, writes=[A["cd"]])
            P.dma(P.pool, A["ident"][:], self.ident_in, writes=[A["cd"]])
            cc = tpc + 256
            P.dma(P.act, A["kc"][:], self.kT_s[:, :, cc:cc + CTX].rearrange("g p n -> p g n"),
                  reads=[self.kv_dep[self.nt_own]], writes=[A["kcd"]])
            P.dma(P.act, A["vc"][:], self.v_s[cc // 128:cc // 128 + 2].rearrange("b p n -> p b n"),
                  reads=[self.kv_dep[self.nt_own]], writes=[A["kcd"]])
            for t in tiles:
                c0 = self.tile_cols(t)
                own = t < self.nt_own
                self.load_x(B, t)
                self.pre_norm(B, t, l * 3 + 1)
                P.dma(P.act, A["cs"][:], self.rope_in[2:4, :, c0:c0 + TT].rearrange("a p n -> p a n"), writes=[A["csd"]])
                if own:
                    kvr = self.kv_dep
                    for (doff, scol, n) in self.win_cols(t):
                        P.dma(P.act, A["kw"][:, :, doff:doff + n], self.kT_s[:, :, scol:scol + n].rearrange("g p n -> p g n"),
                              reads=kvr, writes=[A["kwd"]])
                        P.dma(P.act, A["vw"][:, doff // 128:(doff + n) // 128, :],
                              self.v_s[scol // 128:(scol + n) // 128].rearrange("b p n -> p b n"), reads=kvr, writes=[A["vwd"]])
                for h in range(N_HEADS):
                    bs = []
                    for which in range(2):
                        b = (2 * h + which) % 4
                        P.dma(P.sp, A["wq"][:, b, :], self.aqw_b[h + which * N_HEADS], reads=[self.aw_dep], writes=[A["wqd"][b]])
                        bs.append(b)
                    pa, pb = 0, 1
                    for which, pbk in ((0, pa), (1, pb)):
                        b = bs[which]
                        P.mm_group(self.psum[pbk][:],
                                   [(A["wq"][:, b, k * 128:(k + 1) * 128], B["hx"][:, k, :]) for k in range(KC)],
                                   reads=[A["wqd"][b]] + B["hxd"], writes=[self.psd[pbk]])
                    self.rope_evac(A, pa, pb, A["q"][:, h, :], A["qd"][h], A["cs"][:, 0, :], A["cs"][:, 1, :])
                if self.trunc <= 1:
                    continue
                if own:
                    qbl = []
                    for qb in range(4):
                        mk = 0
                        if t == 0 and qb == 0:
                            mk = 1
                        if t == self.nt_own - 1 and qb == 3:
                            mk = 2
                        qbl.append((qb, mk))
                else:
                    qbl = [(2, 3), (3, 3)]
                    for hh in range(N_HEADS):
                        pass
                self.attn_tile(B, A, t, qbl)
                if self.trunc <= 4:
                    continue
                nq = [q for (q, _) in qbl]
                a_, b_ = nq[0] * 128, nq[-1] * 128 + 128
                if not own:
                    for h in range(N_HEADS):
                        P.op(P.pool, lambda hh, h=h: hh.memset(A["ao"][:, h, 0:256], 0.0), reads=[], writes=[A["aod"][h]])
                for c in range(KC):
                    b = c % 4
                    P.dma(P.sp, A["wq"][:, b, :], self.aow_b[c], reads=[self.aw_dep], writes=[A["wqd"][b]])
                    pb = c % 2
                    P.mm_group(self.psum[pb][:],
                               [(A["wq"][:, b, k * 128:(k + 1) * 128], A["ao"][:, k, :]) for k in range(KC)],
                               reads=[A["wqd"][b]] + A["aod"], writes=[self.psd[pb]])
                    self.sumsq_acc(B, c, self.psum[pb][:], self.psd[pb])
                    P.op(P.dve, lambda h, pb=pb, c=c: h.tensor_copy(out=B["o"][:, c, :], in_=self.psum[pb][:]),
                         reads=[self.psd[pb]], writes=[B["od"][c]])
                if self.trunc <= 5:
                    continue
                self.sumsq_finish(B, 7, B["rstd2"], B["rstd2d"])
                if self.trunc <= 6:
                    continue
                self.post_update(B, t, l * 3 + 1)
                if self.trunc <= 7:
                    continue
                self.store_x(B, t)
            P.barrier()

    def ffn_bufs_small(self, es):
        B = {}
        B["x"] = self.sb("s_x", [128, KC, TT], F32, es)
        B["xd"] = deps(KC, "x")
        B["o"] = self.sb("s_o", [128, KC, TT], BF16, es)
        B["od"] = deps(KC, "o")
        B["hx"] = self.sb("s_hx", [128, KC, TT], BF16, es)
        B["hxd"] = deps(KC, "hx")
        B["sqf"] = self.sb("s_sqf", [128, 2, TT], F32, es)
        B["sqfd"] = deps(2, "sqf")
        B["acc"] = self.sb("s_acc", [128, TT], F32, es)
        B["accd"] = Dep("acc")
        B["xn"] = self.sb("s_xn", [128, 2, TT], F32, es)
        B["xnd"] = deps(2, "xn")
        B["rt"] = self.sb("s_rt", [128, TT], F32, es)
        B["rtd"] = Dep("rt")
        B["rstd"] = self.sb("s_rstd", [128, TT], F32, es)
        B["rstdd"] = Dep("rstd")
        B["rstd2"] = self.sb("s_rstd2", [128, TT], F32, es)
        B["rstd2d"] = Dep("rstd2")
        return B

    def run_phases(self):
        if "launchA" in self.phases:
            self.sweep_ffn_kv(0, 0, list(range(self.ntile)), src=self.x_in)
            self.sweep_attn(list(range(self.ntile)))
            self.sweep_ffn(0, 1, list(range(self.ntile)))
            self.sweep_ffn_u(list(range(self.ntile)))
            return
        if "launchC" in self.phases:
            self.sweep_glu_ffn(list(range(self.ntile)))
            return
        if "modc" in self.phases:
            return
        if "t_ffn00" in self.phases:
            self.sweep_ffn(0, 0, list(range(self.nt_own)), src=self.x_in, final=True)
            return
        if "t_copy" in self.phases:
            P = self.P
            with contextlib.ExitStack() as es:
                B = self.ffn_bufs_small(es)
                for t in range(self.nt_own):
                    self.load_x(B, t, src=self.x_in, src_dep=Dep())
                    if "t_norm" in self.phases:
                        self.pre_norm(B, t, 1)
                        for c in range(KC):
                            P.op(P.dve, lambda h, c=c: h.tensor_copy(out=B["x"][:, c, :], in_=B["hx"][:, c, :]),
                                 reads=[B["hxd"][c]], writes=[B["xd"][c]])
                    self.store_x(B, t, dst=self.y_out, dst_dep=Dep(), is_output=True)
            return
        if "t_attn" in self.phases:
            self.sweep_ffn_kv(0, 0, list(range(self.ntile)), src=self.x_in)
            if "t_noattn" not in self.phases:
                self.sweep_attn(list(range(self.ntile)))
            self.sweep_copy_out()
            return
        raise NotImplementedError

    def sweep_copy_out(self):
        P = self.P
        with contextlib.ExitStack() as es:
            B = {"x": self.sb("c_x", [128, KC, TT], F32, es), "xd": deps(KC)}
            for t in range(self.nt_own):
                self.load_x(B, t)
                self.store_x(B, t, dst=self.y_out, dst_dep=Dep(), is_output=True)
            P.barrier()


def host_common(inputs, tpc, core, ncore_per_batch=4):
    b = core // ncore_per_batch
    j = core % ncore_per_batch
    x = inputs["x"]
    seq = x.shape[1]
    t0 = j * tpc
    own = x[b, t0:t0 + tpc]
    hl = x[b, t0 - 128:t0] if t0 >= 128 else np.zeros((128, D), np.float32)
    hr = x[b, t0 + tpc:t0 + tpc + 128] if t0 + tpc + 128 <= seq else np.zeros((128, D), np.float32)
    ctx = inputs["ctx"][b]
    xin = np.ascontiguousarray(np.concatenate([own, hl, hr, ctx], axis=0).T)
    m = {"xin": xin}
    cm = np.stack([inputs["c"][b], inputs["c_ctx"]], axis=-1)
    m["cm"] = np.ascontiguousarray(cm.reshape(KC, 128, 2).transpose(1, 0, 2))
    return m


def host_shared(inputs):
    m = {}
    for l in range(2):
        m[f"ada{l}"] = tile_w(inputs["ada_w"][l])
        for s in range(2):
            m[f"fwin{l}{s}"] = tile_w(inputs["ffn_w_in"][l, s])
            m[f"fwout{l}{s}"] = tile_w(inputs["ffn_w_out"][l, s])
    ab = inputs["ada_b"]
    m["adab"] = np.ascontiguousarray(ab.reshape(2, N_MOD * KC, 128).transpose(2, 0, 1))
    ng = inputs["norm_g"].reshape(12, KC, 128)
    m["ng"] = np.ascontiguousarray(ng.transpose(2, 0, 1))
    return m


def rope_partner():
    p = np.arange(128)
    return np.where((p % 64) < 32, p + 32, p - 32)


def host_attn_shared(inputs):
    m = {}
    w = inputs["attn_w_in"][0]
    nq = N_HEADS * HEAD_DIM
    nk = N_KV * HEAD_DIM
    wq, wk, wv = w[:, :nq], w[:, nq:nq + nk], w[:, nq + nk:]
    part = rope_partner()
    qperm = (np.arange(N_HEADS)[:, None] * 128 + part[None, :]).reshape(-1)
    kperm = (np.arange(N_KV)[:, None] * 128 + part[None, :]).reshape(-1)
    m["aqw"] = np.concatenate([tile_w(wq), tile_w(wq[:, qperm])], axis=0)
    m["akw"] = np.concatenate([tile_w(wk), tile_w(wk[:, kperm])], axis=0)
    m["avw"] = np.ascontiguousarray(wv.reshape(KC, 128, 512).transpose(1, 0, 2)).reshape(128, KC * 512)
    m["aow"] = tile_w(inputs["attn_w_out"][0])
    m["asink"] = np.ascontiguousarray(np.broadcast_to(inputs["attn_sink"][0][None, :], (128, N_HEADS))).astype(np.float32)
    m["ident"] = np.eye(128, dtype=np.float32)
    return m


def host_attn_core(tpc, core, seq, ncore_per_batch=4):
    j = core % ncore_per_batch
    t0 = j * tpc
    pos = np.concatenate([np.arange(t0, t0 + tpc), np.arange(t0 - 128, t0), np.arange(t0 + tpc, t0 + tpc + 128)])
    pos = np.clip(pos, 0, seq - 1)
    row = (pos // GRID_W).astype(np.float32)
    col = (pos % GRID_W).astype(np.float32)
    inv = (10000.0 ** (-np.arange(0, 64, 2, dtype=np.float32) / 64)).astype(np.float32)
    p = np.arange(128)
    fidx = p % 32
    is_col = p >= 64
    sign = np.where((p % 64) < 32, -1.0, 1.0).astype(np.float32)
    ang = np.where(is_col[:, None], col[None, :] * inv[fidx][:, None], row[None, :] * inv[fidx][:, None]).astype(np.float32)
    cos = np.cos(ang).astype(np.float32)
    sin = (np.sin(ang) * sign[:, None]).astype(np.float32)
    cos = np.concatenate([cos, np.ones((128, CTX), np.float32)], axis=1)
    sin = np.concatenate([sin, np.zeros((128, CTX), np.float32)], axis=1)
    sc = np.float32(HEAD_DIM ** -0.5)
    rope = np.stack([cos, sin, cos * sc, sin * sc], axis=0).astype(np.float32)
    i = np.arange(128)[:, None]
    jj = np.arange(384)[None, :]
    rel = jj - i
    band = (rel >= 0) & (rel <= 256)
    NEG = np.float32(-1e30)
    def mk(valid):
        return np.where(valid, np.float32(0), NEG).astype(np.float32)
    m0 = mk(band)
    m1 = mk(band & (jj >= 128)) if j == 0 else m0
    m2 = mk(band & (jj < 256)) if j == ncore_per_batch - 1 else m0
    m3 = mk(np.zeros_like(band))
    amask = np.ascontiguousarray(np.stack([m0, m1, m2, m3], axis=1))
    return {"rope": rope, "amask": amask}


TWO_PI = 6.283185307179586
CW1 = 6.28125
CW2 = TWO_PI - 6.28125
MAGIC = 12582912.0
NB_SSM = 4


class SSMK:
    def __init__(self, lseq):
        self.lseq = lseq
        self.ntok = CTX + lseq
        self.nch = lseq // TT

    def build(self):
        nc = bass.Bass("TRN2", target_bir_lowering=False)
        self.nc = nc
        self.es = contextlib.ExitStack()
        es = self.es
        P = Prog(nc, es)
        self.P = P
        din = lambda n, s: nc.dram_tensor(n, list(s), F32, kind="ExternalInput").ap()
        self.u_in = din("uT", [NB_SSM * 128, self.ntok])
        self.pc_in = din("pc", [NB_SSM, 2, 128, 2 * 64 + 1 + 2 * 64])
        self.ps_in = din("pst", [NB_SSM, 2, 128, 3 * 8])
        self.cc_in = din("cc", [NB_SSM, 2, 2, 128, 128])
        self.dk_in = din("dk", [128, NB_SSM])
        self.k_in = din("kst", [128, 8 + 8 * 128 + 128 + TT])
        self.g_out = nc.dram_tensor("gT", [NB_SSM * 128, self.lseq], F32, kind="ExternalOutput").ap()
        sb = lambda n, s, dt=F32: es.enter_context(nc.sbuf_tensor(n, list(s), dt))
        self.psum = [es.enter_context(nc.psum_tensor(f"ps{i}", [128, TT], F32)) for i in range(8)]
        self.psd = deps(8, "ps")
        self.kst = sb("kst_sb", [128, 8 + 8 * 128 + 128 + TT])
        self.kd = Dep()
        P.dma(P.sp, self.kst[:], self.k_in, writes=[self.kd])
        self.rowmask = self.kst[:, 0:8]
        self.colmask = self.kst[:, 8:8 + 1024]
        self.swap = self.kst[:, 8 + 1024:8 + 1024 + 128]
        self.iota = self.kst[:, 8 + 1024 + 128:8 + 1024 + 128 + TT]
        self.dk = sb("dk_sb", [128, NB_SSM])
        P.dma(P.sp, self.dk[:], self.dk_in, writes=[self.kd])
        self.ybuf = sb("ybuf", [128, self.lseq])
        self.yd = deps(self.nch, "y")
        self.cos0 = sb("cos0", [128, 8, TT])
        self.sin0 = sb("sin0", [128, 8, TT])
        self.tabd = Dep("tab")
        self.z = sb("z", [128, 8, TT])
        self.zd = deps(8, "z")
        self.zinit = sb("zinit", [128, 8])
        self.zid = Dep("zinit")
        self.w = sb("w", [128, 2, TT])
        self.wd = deps(2, "w")
        self.t1 = sb("t1", [128, 2, TT])
        self.t1d = deps(2, "t1")
        self.Ab = sb("Ab", [128, 8, 2, TT], BF16)
        self.Abd = deps(8, "Ab")
        self.uf = sb("uf", [128, 2, TT])
        self.ufd = deps(2, "uf")
        self.ub = sb("ub", [128, 2, TT], BF16)
        self.ubd = deps(2, "ub")
        self.bp = sb("bp", [128, 2, 8, 128], BF16)
        self.cg = sb("cg", [128, 2, 8, 128], BF16)
        self.setd = Dep("set")
        self.st = sb("st", [128, 8, 8])
        self.sm = sb("sm", [128, 24, 64])
        self.sms = sb("sms", [128, 16, 8])
        self.pcb = sb("pcb", [128, 2 * 64 + 1 + 2 * 64])
        self.psb = sb("psb", [128, 24])
        self.ccb = sb("ccb", [128, 2, 128])
        self.ang = sb("ang", [128, 4, TT])
        self.gt = sb("gt", [128, 4, TT])
        self.gtd = Dep("gt")
        self.zs = sb("zs", [128, 3, 8])
        for bt in range(NB_SSM):
            for dr in range(2):
                self.setup(bt, dr)
                self.scan_dir(bt, dr)
            self.readout(bt)
        P.finish()
        es.close()
        return nc

    def dve(self, fn, reads, writes):
        return self.P.op(self.P.dve, fn, reads=reads, writes=writes)

    def reduce_angle(self, x, out, k, deps_):
        self.dve(lambda h: h.tensor_scalar(k, x, 1.0 / TWO_PI, MAGIC, ALU.mult, ALU.add), deps_, deps_)
        self.dve(lambda h: h.tensor_scalar(k, k, MAGIC, None, ALU.subtract), deps_, deps_)
        self.dve(lambda h: h.scalar_tensor_tensor(out, k, -CW1, x, ALU.mult, ALU.add), deps_, deps_)
        self.dve(lambda h: h.scalar_tensor_tensor(out, k, -CW2, out, ALU.mult, ALU.add), deps_, deps_)

    def sincos(self, x, s_out, c_out, tmp, k, d):
        P = self.P
        self.reduce_angle(x, tmp, k, [d])
        P.op(P.act, lambda h: h.activation(out=s_out, in_=tmp, func=AF.Sin), reads=[d], writes=[d])
        self.dve(lambda h: h.tensor_scalar(tmp, x, TWO_PI / 4, None, ALU.add), [d], [d])
        self.reduce_angle(tmp, tmp, k, [d])
        P.op(P.act, lambda h: h.activation(out=c_out, in_=tmp, func=AF.Sin), reads=[d], writes=[d])

    def setup(self, bt, dr):
        P = self.P
        d = self.setd
        sm, sms = self.sm, self.sms
        S = lambda i: sm[:, i, :]
        P.dma(P.sp, self.pcb[:], self.pc_in[bt, dr], writes=[d])
        P.dma(P.sp, self.psb[:], self.ps_in[bt, dr], writes=[d])
        P.dma(P.sp, self.ccb[:], self.cc_in[bt, dr].rearrange("a p n -> p a n"), writes=[d])
        a_re, a_im, ldt = self.pcb[:, 0:64], self.pcb[:, 64:128], self.pcb[:, 128:129]
        b_re, b_im = self.pcb[:, 129:193], self.pcb[:, 193:257]
        dt = sms[:, 0, 0:1]
        P.op(P.act, lambda h: h.activation(out=dt, in_=ldt, func=AF.Exp), reads=[d], writes=[d])
        self.dve(lambda h: h.tensor_scalar(S(0), a_re, dt, None, ALU.mult), [d], [d])
        P.op(P.act, lambda h: h.activation(out=S(1), in_=S(0), func=AF.Exp), reads=[d], writes=[d])
        self.dve(lambda h: h.tensor_scalar(S(2), a_im, dt, None, ALU.mult), [d], [d])
        self.sincos(S(2), S(3), S(4), S(5), S(6), d)
        self.dve(lambda h: h.tensor_tensor(out=S(7), in0=S(1), in1=S(4), op=ALU.mult), [d], [d])
        self.dve(lambda h: h.tensor_tensor(out=S(8), in0=S(1), in1=S(3), op=ALU.mult), [d], [d])
        self.dve(lambda h: h.tensor_scalar(S(9), S(7), -1.0, None, ALU.add), [d], [d])
        self.dve(lambda h: h.tensor_tensor(out=S(10), in0=a_re, in1=a_re, op=ALU.mult), [d], [d])
        self.dve(lambda h: h.tensor_tensor(out=S(11), in0=a_im, in1=a_im, op=ALU.mult), [d], [d])
        self.dve(lambda h: h.tensor_tensor(out=S(10), in0=S(10), in1=S(11), op=ALU.add), [d], [d])
        self.dve(lambda h: h.reciprocal(out=S(10), in_=S(10)), [d], [d])
        self.dve(lambda h: h.tensor_tensor(out=S(11), in0=S(9), in1=a_re, op=ALU.mult), [d], [d])
        self.dve(lambda h: h.tensor_tensor(out=S(12), in0=S(8), in1=a_im, op=ALU.mult), [d], [d])
        self.dve(lambda h: h.tensor_tensor(out=S(11), in0=S(11), in1=S(12), op=ALU.add), [d], [d])
        self.dve(lambda h: h.tensor_tensor(out=S(11), in0=S(11), in1=S(10), op=ALU.mult), [d], [d])
        self.dve(lambda h: h.tensor_tensor(out=S(12), in0=S(8), in1=a_re, op=ALU.mult), [d], [d])
        self.dve(lambda h: h.tensor_tensor(out=S(13), in0=S(9), in1=a_im, op=ALU.mult), [d], [d])
        self.dve(lambda h: h.tensor_tensor(out=S(12), in0=S(12), in1=S(13), op=ALU.subtract), [d], [d])
        self.dve(lambda h: h.tensor_tensor(out=S(12), in0=S(12), in1=S(10), op=ALU.mult), [d], [d])
        self.dve(lambda h: h.tensor_tensor(out=S(13), in0=S(11), in1=b_re, op=ALU.mult), [d], [d])
        self.dve(lambda h: h.tensor_tensor(out=S(14), in0=S(12), in1=b_im, op=ALU.mult), [d], [d])
        self.dve(lambda h: h.tensor_tensor(out=S(15), in0=S(13), in1=S(14), op=ALU.subtract), [d], [d])
        self.dve(lambda h: h.tensor_tensor(out=S(13), in0=S(11), in1=b_im, op=ALU.mult), [d], [d])
        self.dve(lambda h: h.tensor_tensor(out=S(14), in0=S(12), in1=b_re, op=ALU.mult), [d], [d])
        self.dve(lambda h: h.tensor_tensor(out=S(16), in0=S(13), in1=S(14), op=ALU.add), [d], [d])
        self.dve(lambda h: h.tensor_scalar(S(17), S(15), -1.0, None, ALU.mult), [d], [d])
        for j in range(8):
            mj = self.rowmask[:, j:j + 1]
            for (v, src, half) in ((0, S(15), 0), (0, S(16), 1), (1, S(16), 0), (1, S(17), 1)):
                self.dve(lambda h, v=v, src=src, half=half, j=j, mj=mj: h.tensor_scalar(
                    self.bp[:, v, j, half * 64:(half + 1) * 64], src, mj, None, ALU.mult), [d, self.kd], [d])
        cA, cB = self.ccb[:, 0, :], self.ccb[:, 1, :]
        c1 = sm[:, 18:20, :].rearrange("p a n -> p (a n)")
        c2 = sm[:, 20:22, :].rearrange("p a n -> p (a n)")
        self.dve(lambda h: h.tensor_copy(out=c1[0:64, :], in_=cA[0:64, :]), [d], [d])
        self.dve(lambda h: h.tensor_scalar(c1[64:128, :], cA[64:128, :], -1.0, None, ALU.mult), [d], [d])
        self.dve(lambda h: h.tensor_scalar(c2, cB, -1.0, None, ALU.mult), [d], [d])
        for j in range(8):
            cm = self.colmask[:, j * 128:(j + 1) * 128]
            self.dve(lambda h, j=j, cm=cm: h.tensor_tensor(out=self.cg[:, 0, j, :], in0=c1, in1=cm, op=ALU.mult), [d, self.kd], [d])
            self.dve(lambda h, j=j, cm=cm: h.tensor_tensor(out=self.cg[:, 1, j, :], in0=c2, in1=cm, op=ALU.mult), [d, self.kd], [d])
        a_re_s, a_im_s, ldt_s = self.psb[:, 0:8], self.psb[:, 8:16], self.psb[:, 16:24]
        T = lambda i: sms[:, i, :]
        P.op(P.act, lambda h: h.activation(out=T(1), in_=ldt_s, func=AF.Exp), reads=[d], writes=[d])
        self.dve(lambda h: h.tensor_tensor(out=T(2), in0=a_re_s, in1=T(1), op=ALU.mult), [d], [d])
        P.op(P.act, lambda h: h.activation(out=self.st[:, 0, :], in_=T(2), func=AF.Exp), reads=[d], writes=[d])
        self.dve(lambda h: h.tensor_tensor(out=self.st[:, 1, :], in0=a_im_s, in1=T(1), op=ALU.mult), [d], [d])
        for (n, ci) in ((TT, 2), (CTX, 4)):
            self.dve(lambda h, n=n: h.tensor_scalar(T(3), self.st[:, 1, :], float(n), None, ALU.mult), [d], [d])
            self.sincos(T(3), self.st[:, ci + 1, :], self.st[:, ci, :], T(4), T(5), d)
        td = self.tabd
        for g in range(8):
            a0, a1, a2 = self.ang[:, 0, :], self.ang[:, 1, :], self.ang[:, 2, :]
            self.dve(lambda h, g=g: h.tensor_scalar(a0, self.iota, self.st[:, 1, g:g + 1], None, ALU.mult), [d, self.kd, td], [td])
            self.sincos(a0, self.sin0[:, g, :], self.cos0[:, g, :], a1, a2, td)

    def chunk_list(self, dr):
        lat = [(CTX + i * TT, TT, False, i) for i in range(self.nch)]
        if dr == 0:
            return [(0, CTX, True, -1)] + lat
        return [(0, CTX, True, -1)] + lat[::-1]

    def rv(self, ap, n, rev):
        a = ap[:, 0:n]
        if not rev:
            return a
        return bass.AP(a.tensor, a.offset + (n - 1) * a.ap[-1][0], [list(a.ap[0]), [-a.ap[-1][0], n]])

    def scan_dir(self, bt, dr):
        P = self.P
        rev = dr == 1
        d = self.setd
        td = self.tabd
        self.dve(lambda h: h.memset(self.zinit[:], 0.0), [], [self.zid])
        for ci, (col0, n, is_ctx, chi) in enumerate(self.chunk_list(dr)):
            ui = ci % 2
            P.dma(P.sp, self.uf[:, ui, 0:n], self.u_in[bt * 128:(bt + 1) * 128, col0:col0 + n], writes=[self.ufd[ui]])
            P.op(P.act, lambda h, ui=ui, n=n: h.activation(out=self.ub[:, ui, 0:n], in_=self.uf[:, ui, 0:n], func=AF.Copy),
                 reads=[self.ufd[ui]], writes=[self.ubd[ui]])
            for g0 in range(0, 8, 2):
                gs = (g0, g0 + 1)
                ctxs = {}
                for g in gs:
                    p1, p2 = 2 * (g % 2), 2 * (g % 2) + 1
                    for v, pb in ((0, p1), (1, p2)):
                        P.mm_group(self.psum[pb][:, 0:n], [(self.bp[:, v, g, :], self.ub[:, ui, 0:n])],
                                   reads=[d, self.ubd[ui]], writes=[self.psd[pb]])
                    wi = g % 2
                    ctxs[g] = dict(p1=p1, p2=p2, wi=wi,
                                   cosT=self.rv(self.cos0[:, g, :], n, rev), sinT=self.rv(self.sin0[:, g, :], n, rev),
                                   w=self.w[:, wi, 0:n], t1=self.t1[:, wi, 0:n], zg=self.z[:, g, :])
                for g in gs:
                    c = ctxs[g]
                    self.dve(lambda h, c=c: h.tensor_tensor(out=c["t1"], in0=self.psum[c["p2"]][:, 0:n], in1=c["sinT"], op=ALU.mult),
                             [self.psd[c["p2"]], td], [self.t1d[c["wi"]]])
                    self.dve(lambda h, c=c: h.tensor_tensor(out=c["w"], in0=self.psum[c["p1"]][:, 0:n], in1=c["cosT"], op=ALU.mult),
                             [self.psd[c["p1"]], td], [self.wd[c["wi"]]])
                for g in gs:
                    c = ctxs[g]
                    self.dve(lambda h, c=c: h.tensor_tensor(out=c["w"], in0=c["w"], in1=c["t1"], op=ALU.add),
                             [self.t1d[c["wi"]], self.wd[c["wi"]]], [self.wd[c["wi"]]])
                for g in gs:
                    c = ctxs[g]
                    self.dve(lambda h, g=g, c=c: h.tensor_tensor_scan(
                        out=self.rv(c["zg"], n, rev), data0=self.st[:, 0, g:g + 1].to_broadcast([128, n]),
                        data1=self.rv(c["w"], n, rev), initial=self.zinit[:, g:g + 1], op0=ALU.mult, op1=ALU.add),
                             [self.wd[c["wi"]], d, self.zid], [self.zd[g]])
                if not is_ctx:
                    for g in gs:
                        c = ctxs[g]
                        P.op(P.pool, lambda h, g=g, c=c: h.tensor_tensor(
                            out=self.Ab[:, g, 0, 0:n], in0=c["zg"][:, 0:n], in1=c["cosT"], op=ALU.mult),
                             reads=[self.zd[g], td], writes=[self.Abd[g]])
                        P.op(P.pool, lambda h, g=g, c=c: h.tensor_tensor(
                            out=self.Ab[:, g, 1, 0:n], in0=c["zg"][:, 0:n], in1=c["sinT"], op=ALU.mult),
                             reads=[self.zd[g], td], writes=[self.Abd[g]])
            last = 0 if rev else n - 1
            zl = self.z[:, :, last]
            ca, sa = (self.st[:, 2, :], self.st[:, 3, :]) if n == TT else (self.st[:, 4, :], self.st[:, 5, :])
            self.dve(lambda h, zl=zl: h.tensor_copy(out=self.zs[:, 0, :], in_=zl), self.zd, [self.zid])
            P.mm_group(self.psum[6][:, 0:8], [(self.swap, self.zs[:, 0, :])], reads=[self.zid, self.kd], writes=[self.psd[6]])
            self.dve(lambda h, sa=sa: h.tensor_tensor(out=self.zs[:, 1, :], in0=self.psum[6][:, 0:8], in1=sa, op=ALU.mult),
                     [self.psd[6], d], [self.zid])
            self.dve(lambda h, ca=ca: h.tensor_tensor(out=self.zs[:, 2, :], in0=self.zs[:, 0, :], in1=ca, op=ALU.mult),
                     [self.zid, d], [self.zid])
            self.dve(lambda h: h.tensor_tensor(out=self.zinit[:], in0=self.zs[:, 1, :], in1=self.zs[:, 2, :], op=ALU.add),
                     [self.zid], [self.zid])
            if not is_ctx:
                pb = 4 + ci % 2
                pairs = []
                for g in range(8):
                    pairs.append((self.cg[:, 0, g, :], self.Ab[:, g, 0, 0:n]))
                    pairs.append((self.cg[:, 1, g, :], self.Ab[:, g, 1, 0:n]))
                P.mm_group(self.psum[pb][:, 0:n], pairs, reads=self.Abd + [d], writes=[self.psd[pb]])
                yc = self.ybuf[:, chi * TT:(chi + 1) * TT]
                if dr == 0:
                    P.op(P.act, lambda h, yc=yc, pb=pb: h.activation(out=yc, in_=self.psum[pb][:], func=AF.Copy),
                         reads=[self.psd[pb]], writes=[self.yd[chi]])
                else:
                    self.dve(lambda h, yc=yc, pb=pb: h.tensor_tensor(out=yc, in0=yc, in1=self.psum[pb][:], op=ALU.add),
                             [self.psd[pb], self.yd[chi]], [self.yd[chi]])

    def readout(self, bt):
        P = self.P
        C0 = 0.7978845608028654
        for chi in range(self.nch):
            ui = chi % 2
            col0 = CTX + chi * TT
            P.dma(P.sp, self.uf[:, ui, :], self.u_in[bt * 128:(bt + 1) * 128, col0:col0 + TT], writes=[self.ufd[ui]])
            yc = self.ybuf[:, chi * TT:(chi + 1) * TT]
            t, t2, p, o = self.gt[:, 0, :], self.gt[:, 1, :], self.gt[:, 2, :], self.gt[:, 3, :]
            gd = self.gtd
            self.dve(lambda h, ui=ui, yc=yc: h.scalar_tensor_tensor(t, self.uf[:, ui, :], self.dk[:, bt:bt + 1], yc, ALU.mult, ALU.add),
                     [self.ufd[ui], self.yd[chi], self.kd, gd], [gd])
            self.dve(lambda h: h.tensor_tensor(out=t2, in0=t, in1=t, op=ALU.mult), [gd], [gd])
            self.dve(lambda h: h.tensor_scalar(t2, t2, 0.044715, 1.0, ALU.mult, ALU.add), [gd], [gd])
            self.dve(lambda h: h.tensor_tensor(out=p, in0=t2, in1=t, op=ALU.mult), [gd], [gd])
            P.op(P.act, lambda h: h.activation(out=p, in_=p, func=AF.Tanh, scale=C0), reads=[gd], writes=[gd])
            self.dve(lambda h: h.tensor_scalar(p, p, 1.0, 0.5, ALU.add, ALU.mult), [gd], [gd])
            self.dve(lambda h: h.tensor_tensor(out=o, in0=p, in1=t, op=ALU.mult), [gd], [gd])
            P.dma(P.sp, self.g_out[bt * 128:(bt + 1) * 128, chi * TT:(chi + 1) * TT], o, reads=[gd], writes=[self.yd[chi]],
                  is_output=True)


def host_ssm(inputs, u_full, core):
    b, q = core // 4, core % 4
    m = {"uT": np.ascontiguousarray(u_full[b, q * 512:(q + 1) * 512])}
    g0 = q * 32
    pc = np.zeros((NB_SSM, 2, 128, 257), np.float32)
    pst = np.zeros((NB_SSM, 2, 128, 24), np.float32)
    cc = np.zeros((NB_SSM, 2, 2, 128, 128), np.float32)
    for bt in range(NB_SSM):
        gs = slice(g0 + bt * 8, g0 + bt * 8 + 8)
        for dr in range(2):
            a_re = inputs["ssm_a_re"][0, dr, gs]
            a_im = inputs["ssm_a_im"][0, dr, gs]
            ldt = inputs["ssm_log_dt"][0, dr, gs]
            b_re = inputs["ssm_b_re"][0, dr, gs]
            b_im = inputs["ssm_b_im"][0, dr, gs]
            c_re = inputs["ssm_c_re"][0, dr, gs]
            c_im = inputs["ssm_c_im"][0, dr, gs]
            pc[bt, dr, :, 0:64] = np.repeat(a_re, 16, axis=0)
            pc[bt, dr, :, 64:128] = np.repeat(a_im, 16, axis=0)
            pc[bt, dr, :, 128] = np.repeat(ldt, 16)
            pc[bt, dr, :, 129:193] = b_re.transpose(0, 2, 1).reshape(128, 64)
            pc[bt, dr, :, 193:257] = b_im.transpose(0, 2, 1).reshape(128, 64)
            pst[bt, dr, :, 0:8] = np.concatenate([a_re.T, a_re.T], axis=0)
            pst[bt, dr, :, 8:16] = np.concatenate([a_im.T, a_im.T], axis=0)
            pst[bt, dr, :, 16:24] = np.broadcast_to(ldt[None, :], (128, 8))
            crT = c_re.transpose(2, 0, 1).reshape(64, 128)
            ciT = c_im.transpose(2, 0, 1).reshape(64, 128)
            cc[bt, dr, 0] = np.concatenate([crT, ciT], axis=0)
            cc[bt, dr, 1] = np.concatenate([ciT, crT], axis=0)
    m["pc"], m["pst"], m["cc"] = pc, pst, cc
    m["dk"] = np.ascontiguousarray(inputs["ssm_d"][0, q * 512:(q + 1) * 512].reshape(NB_SSM, 128).T)
    kst = np.zeros((128, 8 + 1024 + 128 + TT), np.float32)
    rows = np.arange(128)
    for j in range(8):
        kst[:, j] = (rows // 16 == j)
        kst[:, 8 + j * 128:8 + (j + 1) * 128] = (np.arange(128)[None, :] // 16 == j)
    sw = np.zeros((128, 128), np.float32)
    for mm in range(64):
        sw[64 + mm, mm] = -1.0
        sw[mm, 64 + mm] = 1.0
    kst[:, 8 + 1024:8 + 1024 + 128] = sw
    kst[:, 8 + 1024 + 128:] = np.arange(TT, dtype=np.float32)[None, :]
    m["kst"] = kst
    return m


def run_all(inputs, tpc):
    inputs = {k: np.asarray(v) for k, v in inputs.items()}
    x = inputs["x"]
    nb, seq, _ = x.shape
    ncb = seq // tpc
    ncores = nb * ncb
    assert ncores == 8
    cores = list(range(ncores))
    NJ = N_MOD * KC // 8
    mkM = MK(512, phases=["modc", "nospecial"])
    ncM = mkM.build()
    c3 = np.stack([inputs["c"][0], inputs["c"][1], inputs["c_ctx"]], axis=-1)
    cm3 = np.ascontiguousarray(c3.reshape(KC, 128, 3).transpose(1, 0, 2))
    ada_t = [tile_w(inputs["ada_w"][l]) for l in range(2)]
    adab = np.ascontiguousarray(inputs["ada_b"].reshape(2, N_MOD * KC, 128).transpose(2, 0, 1))
    mapsM = []
    for c in cores:
        sl = slice(c * NJ, (c + 1) * NJ)
        mapsM.append({"cm3": cm3, "ada0": np.ascontiguousarray(ada_t[0][sl]), "ada1": np.ascontiguousarray(ada_t[1][sl]),
                      "adab": np.ascontiguousarray(adab[:, :, sl])})
    rM = run_bass_kernel_spmd(ncM, mapsM, core_ids=cores)
    modraw = np.concatenate([rM.results[c]["modraw"] for c in cores], axis=2)
    del ada_t, mapsM
    ng = np.ascontiguousarray(inputs["norm_g"].reshape(12, KC, 128).transpose(2, 0, 1))

    def modr2(b):
        return np.ascontiguousarray(modraw[:, :, :, [b, 2]])

    mkA = MK(tpc, phases=["modd", "attn", "ffn00", "ffn01", "ffn10", "ssmu", "launchA"])
    ncA = mkA.build()
    shA = host_attn_shared(inputs)
    for (l, s) in ((0, 0), (0, 1), (1, 0)):
        shA[f"fwin{l}{s}"] = tile_w(inputs["ffn_w_in"][l, s])
        shA[f"fwout{l}{s}"] = tile_w(inputs["ffn_w_out"][l, s])
    shA["suw"] = tile_w(inputs["ssm_w_in"][0])
    shA["ng"] = ng
    mapsA = []
    for c in cores:
        m = dict(shA)
        m["xin"] = host_common(inputs, tpc, c, ncb)["xin"]
        m.update(host_attn_core(tpc, c, seq, ncb))
        m["modr2"] = modr2(c // ncb)
        mapsA.append(m)
    rA = run_bass_kernel_spmd(ncA, mapsA, core_ids=cores)
    x4 = [rA.results[c]["x4"] for c in cores]
    uo = [rA.results[c]["uTo"] for c in cores]
    del mapsA, shA
    u_full = np.zeros((nb, D, CTX + seq), np.float32)
    for b in range(nb):
        u_full[b, :, 0:CTX] = uo[b * ncb][:, tpc + 256:tpc + 512]
        for j in range(ncb):
            u_full[b, :, CTX + j * tpc:CTX + (j + 1) * tpc] = uo[b * ncb + j][:, 0:tpc]
    skB = SSMK(seq)
    ncB = skB.build()
    mapsB = [host_ssm(inputs, u_full, c) for c in cores]
    rB = run_bass_kernel_spmd(ncB, mapsB, core_ids=cores)
    g_full = np.zeros((nb, D, seq), np.float32)
    for c in cores:
        b, q = c // 4, c % 4
        g_full[b, q * 512:(q + 1) * 512] = rB.results[c]["gT"]
    del mapsB, u_full
    mkC = MK(tpc, phases=["modd", "glu", "ffn11", "yout", "nospecial", "launchC"])
    ncC = mkC.build()
    shC = {"gw": tile_w(inputs["ssm_w_glu"][0]), "fwin11": tile_w(inputs["ffn_w_in"][1, 1]),
           "fwout11": tile_w(inputs["ffn_w_out"][1, 1]), "ng": ng}
    mapsC = []
    for c in cores:
        b, j = c // ncb, c % ncb
        m = dict(shC)
        m["xin"] = np.ascontiguousarray(x4[c][:, 0:tpc])
        m["gin"] = np.ascontiguousarray(g_full[b, :, j * tpc:(j + 1) * tpc])
        m["modr2"] = modr2(b)
        mapsC.append(m)
    rC = run_bass_kernel_spmd(ncC, mapsC, core_ids=cores)
    out = np.zeros((nb, seq, D), np.float32)
    for c in cores:
        b, j = c // ncb, c % ncb
        out[b, j * tpc:(j + 1) * tpc] = rC.results[c]["yout"].T
    return out


def kernel(**inputs):
    return run_all(inputs, 4096)
```

```python
import contextlib
import numpy as np
import concourse.bass as bass
import concourse.mybir as mybir
from concourse.bass_utils import run_bass_kernel_spmd

F32 = mybir.dt.float32
BF16 = mybir.dt.bfloat16
AF = mybir.ActivationFunctionType
ALU = mybir.AluOpType


class Ev:
    __slots__ = ("sem", "val", "key")

    def __init__(self, sem, val, key):
        self.sem = sem
        self.val = val
        self.key = key


class Dep:
    __slots__ = ("w", "r", "name")

    def __init__(self, name=""):
        self.w = None
        self.r = {}
        self.name = name


def deps(n, name=""):
    return [Dep(f"{name}{i}") for i in range(n)]


class Eng:
    def __init__(self, name, h, sem):
        self.name = name
        self.h = h
        self.sem = sem
        self.cnt = 0
        self.waited = {}


class Slot:
    __slots__ = ("sem", "val", "key")

    def __init__(self, sem, key):
        self.sem = sem
        self.val = 0
        self.key = key


class Prog:
    NSLOT = 12

    def __init__(self, nc, es):
        self.nc = nc
        self.es = es
        mk = lambda n, h: Eng(n, h, es.enter_context(nc.semaphore("sem_" + n)))
        self.pe = mk("pe", nc.tensor)
        self.act = mk("act", nc.scalar)
        self.dve = mk("dve", nc.vector)
        self.pool = mk("pool", nc.gpsimd)
        self.sp = mk("sp", nc.sync)
        self.engs = [self.pe, self.act, self.dve, self.pool, self.sp]
        self.slots = {}
        self.scur = {}
        for q in (self.sp, self.pool, self.act):
            self.slots[q.name] = [
                Slot(es.enter_context(nc.semaphore(f"dq_{q.name}{i}")), f"dq_{q.name}{i}")
                for i in range(self.NSLOT)
            ]
            self.scur[q.name] = 0
        self.out_evs = []
        self.pool_out = []
        self.n_inst = 0

    def wait(self, eng, ev):
        if ev is None:
            return
        if eng.name == "pe" and ev.key == "e_pe":
            return
        if eng.waited.get(ev.key, 0) >= ev.val:
            return
        eng.h.wait_ge(ev.sem, ev.val)
        eng.waited[ev.key] = ev.val
        self.n_inst += 1

    def sync_for(self, eng, reads, writes):
        for d in reads:
            self.wait(eng, d.w)
        for d in writes:
            self.wait(eng, d.w)
            for ev in d.r.values():
                self.wait(eng, ev)

    def commit(self, ev, reads, writes):
        for d in reads:
            old = d.r.get(ev.key)
            if old is None or old.val < ev.val:
                d.r[ev.key] = ev
        for d in writes:
            d.w = ev
            d.r = {}

    def op(self, eng, fn, reads=(), writes=()):
        self.sync_for(eng, reads, writes)
        inst = fn(eng.h)
        eng.cnt += 1
        inst.then_inc(eng.sem, 1)
        ev = Ev(eng.sem, eng.cnt, "e_" + eng.name)
        self.commit(ev, reads, writes)
        self.n_inst += 1
        return ev

    def mm_group(self, out, pairs, reads, writes, **kw):
        pe = self.pe
        self.sync_for(pe, reads, writes)
        n = len(pairs)
        inst = None
        for i, (l, r) in enumerate(pairs):
            inst = pe.h.matmul(out, l, r, start=(i == 0), stop=(i == n - 1), **kw)
        pe.cnt += 1
        inst.then_inc(pe.sem, 1)
        ev = Ev(pe.sem, pe.cnt, "e_pe")
        self.commit(ev, reads, writes)
        self.n_inst += n
        return ev

    def dma(self, q, out, in_, reads=(), writes=(), is_output=False, **kw):
        if q.name == "pool":
            rows = 1
            for d in list(out.shape)[:-1]:
                rows *= int(d)
            last_b = int(list(in_.shape)[-1]) * 4
            nd = rows * max(1, -(-last_b // 8192)) // 16 + 2
            while self.pool_out and sum(n for _, n in self.pool_out) + nd > 600:
                ev0, _ = self.pool_out.pop(0)
                self.wait(q, ev0)
        sl = self.slots[q.name]
        slot = sl[self.scur[q.name] % len(sl)]
        self.scur[q.name] += 1
        if slot.val > 0:
            self.wait(q, Ev(slot.sem, slot.val, slot.key))
        self.sync_for(q, reads, writes)
        q.h.dma_start(out=out, in_=in_, **kw).then_inc(slot.sem, 16)
        slot.val += 16
        ev = Ev(slot.sem, slot.val, slot.key)
        self.commit(ev, reads, writes)
        if is_output:
            self.out_evs.append(ev)
        if q.name == "pool":
            self.pool_out.append((ev, nd))
        self.n_inst += 1
        return ev

    def barrier(self):
        evs = [Ev(e.sem, e.cnt, "e_" + e.name) for e in self.engs if e.cnt > 0]
        for sl in self.slots.values():
            for s in sl:
                if s.val > 0:
                    evs.append(Ev(s.sem, s.val, s.key))
        for e in self.engs:
            for ev in evs:
                self.wait(e, ev)

    def finish(self):
        evs = []
        for sl in self.slots.values():
            for s in sl:
                if s.val > 0:
                    evs.append(Ev(s.sem, s.val, s.key))
        for e in self.engs:
            if e.cnt > 0:
                evs.append(Ev(e.sem, e.cnt, "e_" + e.name))
        for ev in evs:
            self.wait(self.sp, ev)


D = 2048
F = 5632
KC = D // 128
FC = F // 128
TT = 512
EPS = 1e-6
N_MOD = 9
HEAD_DIM = 128
N_HEADS = 16
N_KV = 4
GRID_W = 64
CTX = 256


def tile_w(W):
    K, N = W.shape
    return np.ascontiguousarray(
        W.reshape(K // 128, 128, N // 128, 128).transpose(2, 1, 0, 3)
    ).reshape(N // 128, 128, (K // 128) * 128)


def fm(v):
    v = np.asarray(v)
    lead = v.shape[:-1]
    n = v.shape[-1] // 128
    a = v.reshape(lead + (n, 128))
    a = np.moveaxis(a, -1, 0)
    return np.ascontiguousarray(a)


class MK:
    def __init__(self, tpc, phases=None, dbg=None, skip_ffn=False):
        self.skip_ffn = skip_ffn
        import os
        self.trunc = int(os.environ.get("MK_TRUNC", "99"))
        self.tpc = tpc
        self.nt_own = tpc // TT
        self.special = "nospecial" not in (phases or [])
        self.ntile = self.nt_own + (1 if self.special else 0)
        self.ntok = tpc + (TT if self.special else 0)
        self.phases = phases
        self.dbg = dbg or []

    def build(self):
        nc = bass.Bass("TRN2", target_bir_lowering=False)
        self.nc = nc
        self.es = contextlib.ExitStack()
        es = self.es
        P = Prog(nc, es)
        self.P = P
        self.declare_io()
        self.alloc_common()
        self.ph_consts()
        self.ph_convert()
        if self.want("modin"):
            self.ph_modin()
        elif self.want("modc"):
            self.ph_mod_compute()
        elif self.want("modd"):
            self.ph_mod_derive()
        self.run_phases()
        P.finish()
        es.close()
        return nc

    def din(self, name, shape, dt=F32):
        return self.nc.dram_tensor(name, list(shape), dt, kind="ExternalInput").ap()

    def dscr(self, name, shape, dt=F32):
        return self.nc.dram_tensor(name, list(shape), dt).ap()

    def sb(self, name, shape, dt=F32, es=None):
        self._uid = getattr(self, "_uid", 0) + 1
        return (es or self.es).enter_context(self.nc.sbuf_tensor(f"{name}_{self._uid}", list(shape), dt))

    def declare_io(self):
        ntok = self.ntok
        if not self.want("modc"):
            self.x_in = self.din("xin", [D, ntok])
        self.fwin_in = [[None] * 2 for _ in range(2)]
        self.fwout_in = [[None] * 2 for _ in range(2)]
        self.fwin_b = [[None] * 2 for _ in range(2)]
        self.fwout_b = [[None] * 2 for _ in range(2)]
        for l in range(2):
            for s in range(2):
                if self.want(f"ffn{l}{s}"):
                    self.fwin_in[l][s] = self.din(f"fwin{l}{s}", [2 * FC, 128, D])
                    self.fwout_in[l][s] = self.din(f"fwout{l}{s}", [KC, 128, F])
                    self.fwin_b[l][s] = self.dscr(f"fwinb{l}{s}", [2 * FC, 128, D], BF16)
                    self.fwout_b[l][s] = self.dscr(f"fwoutb{l}{s}", [KC, 128, F], BF16)
        self.fwin_dep = [[Dep() for s in range(2)] for l in range(2)]
        self.fwout_dep = [[Dep() for s in range(2)] for l in range(2)]
        self.xs = self.dscr("xs", [D, ntok])
        self.xs_dep = deps(self.ntile, "xs")
        if self.want("yout"):
            self.y_out = self.nc.dram_tensor("yout", [D, self.tpc], F32, kind="ExternalOutput").ap()
        if self.want("ssmu"):
            self.suw_in = self.din("suw", [KC, 128, D])
            self.suw_b = self.dscr("suwb", [KC, 128, D], BF16)
            self.suw_dep = Dep()
            self.x4_out = self.nc.dram_tensor("x4", [D, ntok], F32, kind="ExternalOutput").ap()
            self.u_out = self.nc.dram_tensor("uTo", [D, ntok], F32, kind="ExternalOutput").ap()
        if self.want("glu"):
            self.gw_in = self.din("gw", [2 * KC, 128, D])
            self.gw_b = self.dscr("gwb", [2 * KC, 128, D], BF16)
            self.gw_dep = Dep()
            self.g_in = self.din("gin", [D, ntok])
        if self.want("attn"):
            self.attn_decl()
        if self.want("modin"):
            self.modin = [self.din(n, [128, 6, KC, 2]) for n in ("modA", "modB", "modG")]
        self.dbg_out = {}
        for name, shape in self.dbg:
            self.dbg_out[name] = self.nc.dram_tensor("dbg_" + name, list(shape), F32, kind="ExternalOutput").ap()

    def alloc_common(self):
        nc = self.nc
        es = self.es
        self.psum = [es.enter_context(nc.psum_tensor(f"ps{i}", [128, TT], F32)) for i in range(8)]
        self.psd = deps(8, "ps")
        self.ones = self.sb("ones", [128, 128], BF16)
        self.ones_d = Dep("ones")
        self.epsc = self.sb("epsc", [128, 1])
        self.ones32 = self.sb("ones32", [128, 128], F32)
        self.modA = self.sb("modA_sb", [128, 6, KC, 2])
        self.modB = self.sb("modB_sb", [128, 6, KC, 2])
        self.modG = self.sb("modG_sb", [128, 6, KC, 2])
        self.mod_d = Dep("mod")

    def ph_consts(self):
        P = self.P
        P.op(P.dve, lambda h: h.memset(self.ones[:], 1.0), writes=[self.ones_d])
        P.op(P.dve, lambda h: h.memset(self.epsc[:], EPS), writes=[self.ones_d])
        P.op(P.dve, lambda h: h.memset(self.ones32[:], 1.0), writes=[self.ones_d])

    def convert(self, src, dst, dep, grp=8):
        P = self.P
        n = src.shape[0]
        for j0 in range(0, n, grp):
            j1 = min(n, j0 + grp)
            if j1 - j0 == 1:
                P.dma(P.pool, dst[j0], src[j0], writes=[dep], max_dma_last_dim=8192)
            else:
                P.dma(P.pool, dst[j0:j1].rearrange("j p e -> p j e"), src[j0:j1].rearrange("j p e -> p j e"),
                      writes=[dep], max_dma_last_dim=8192)

    def ph_convert(self):
        done = set()

        def ffn(l, s_):
            if self.want(f"ffn{l}{s_}") and (l, s_) not in done:
                done.add((l, s_))
                self.convert(self.fwin_in[l][s_], self.fwin_b[l][s_], self.fwin_dep[l][s_])
                self.convert(self.fwout_in[l][s_], self.fwout_b[l][s_], self.fwout_dep[l][s_], grp=1)

        if self.want("glu"):
            self.convert(self.gw_in, self.gw_b, self.gw_dep)
        ffn(0, 0)
        if self.want("attn"):
            self.attn_convert()
        for l in range(2):
            for s_ in range(2):
                ffn(l, s_)
        if self.want("ssmu"):
            self.convert(self.suw_in, self.suw_b, self.suw_dep)

    def ph_modin(self):
        P = self.P
        for src, dst in zip(self.modin, (self.modA, self.modB, self.modG)):
            P.dma(P.sp, dst[:], src, writes=[self.mod_d])

    def want(self, ph):
        return ph in self.phases

    def ph_mod_compute(self):
        P = self.P
        NJ = N_MOD * KC // 8
        cm_in = self.din("cm3", [128, KC, 3])
        ada_in = [self.din(f"ada{l}", [NJ, 128, D]) for l in range(2)]
        adab_in = self.din("adab", [128, 2, NJ])
        out = self.nc.dram_tensor("modraw", [128, 2, NJ, 3], F32, kind="ExternalOutput").ap()
        with contextlib.ExitStack() as es:
            cm = self.sb("cm_sb", [128, KC, 3], es=es)
            cs = self.sb("cs_sb", [128, KC, 3], es=es)
            adab = self.sb("adab_sb", [128, 2, NJ], es=es)
            modr = self.sb("modr", [128, 2, NJ, 3], es=es)
            wb = [self.sb(f"adaw{i}", [128, 2, D], es=es) for i in range(2)]
            wbd = deps(2, "adaw")
            d_cm, d_cs, d_adab, d_modr = Dep(), Dep(), Dep(), Dep()
            P.dma(P.sp, cm[:], cm_in, writes=[d_cm])
            P.dma(P.sp, adab[:], adab_in, writes=[d_adab])
            P.op(P.act, lambda h: h.activation(out=cs[:], in_=cm[:], func=AF.Silu), reads=[d_cm], writes=[d_cs])
            it = 0
            for l in range(2):
                for j0 in range(0, NJ, 2):
                    b = it % 2
                    it += 1
                    P.dma(P.sp, wb[b][:], ada_in[l][j0:j0 + 2].rearrange("j p e -> p j e"), writes=[wbd[b]])
                    for jj in range(2):
                        j = j0 + jj
                        pb = j % 2
                        ps = self.psum[pb]
                        P.mm_group(ps[:, 0:3],
                                   [(wb[b][:, jj, k * 128:(k + 1) * 128], cs[:, k, :]) for k in range(KC)],
                                   reads=[wbd[b], d_cs], writes=[self.psd[pb]])
                        P.op(P.dve, lambda h, ps=ps, l=l, j=j: h.tensor_tensor(
                            out=modr[:, l, j, :], in0=ps[:, 0:3], in1=adab[:, l, j:j + 1].to_broadcast([128, 3]),
                            op=ALU.add), reads=[self.psd[pb], d_adab], writes=[d_modr])
            P.dma(P.sp, out, modr[:], reads=[d_modr], writes=[Dep()], is_output=True)
            P.barrier()

    def ph_mod_derive(self):
        P = self.P
        modr_in = self.din("modr2", [128, 2, N_MOD * KC, 2])
        ng_in = self.din("ng", [128, 12, KC])
        with contextlib.ExitStack() as es:
            modr = self.sb("modr", [128, 2, N_MOD * KC, 2], es=es)
            ng = self.sb("ng_sb", [128, 12, KC], es=es)
            d_modr, d_ng = Dep(), Dep()
            P.dma(P.sp, modr[:], modr_in, writes=[d_modr])
            P.dma(P.sp, ng[:], ng_in, writes=[d_ng])
            for l in range(2):
                for s3 in range(3):
                    idx = l * 3 + s3
                    sh = modr[:, l, (3 * s3) * KC:(3 * s3 + 1) * KC, :]
                    sc = modr[:, l, (3 * s3 + 1) * KC:(3 * s3 + 2) * KC, :]
                    gt = modr[:, l, (3 * s3 + 2) * KC:(3 * s3 + 3) * KC, :]
                    gpre = ng[:, l * 6 + 2 * s3, :]
                    gpost = ng[:, l * 6 + 2 * s3 + 1, :]
                    wgt = 1.0 if s3 == 1 else 0.5
                    for r in range(2):
                        P.op(P.dve, lambda h, sc=sc, gpre=gpre, idx=idx, r=r: h.scalar_tensor_tensor(
                            out=self.modA[:, idx, :, r], in0=sc[:, :, r], scalar=1.0, in1=gpre,
                            op0=ALU.add, op1=ALU.mult), reads=[d_modr, d_ng], writes=[self.mod_d])
                        P.op(P.dve, lambda h, sh=sh, idx=idx, r=r: h.tensor_copy(
                            out=self.modB[:, idx, :, r], in_=sh[:, :, r]), reads=[d_modr], writes=[self.mod_d])
                        P.op(P.dve, lambda h, gt=gt, gpost=gpost, idx=idx, r=r, wgt=wgt: h.scalar_tensor_tensor(
                            out=self.modG[:, idx, :, r], in0=gt[:, :, r], scalar=wgt, in1=gpost,
                            op0=ALU.mult, op1=ALU.mult), reads=[d_modr, d_ng], writes=[self.mod_d])
            P.barrier()

    def tile_cols(self, t):
        return t * TT

    def segs(self, t):
        if t < self.nt_own:
            return [(0, TT, 0)]
        return [(0, TT // 2, 0), (TT // 2, TT, 1)]

    def sumsq_acc(self, B, c, src_ap, src_dep):
        P = self.P
        i = c % 2
        P.op(P.act, lambda h: h.activation(out=B["sqf"][:, i, :], in_=src_ap, func=AF.Square),
             reads=[src_dep], writes=[B["sqfd"][i]])
        if c == 0:
            P.op(P.dve, lambda h: h.tensor_copy(out=B["acc"][:], in_=B["sqf"][:, i, :]),
                 reads=[B["sqfd"][i]], writes=[B["accd"]])
        else:
            P.op(P.dve, lambda h: h.tensor_tensor(out=B["acc"][:], in0=B["acc"][:], in1=B["sqf"][:, i, :], op=ALU.add),
                 reads=[B["sqfd"][i], B["accd"]], writes=[B["accd"]])

    def sumsq_finish(self, B, bank, rstd, rstd_d):
        P = self.P
        P.op(P.pe, lambda h: h.matmul(self.psum[bank][:], self.ones32[:], B["acc"][:], start=True, stop=True),
             reads=[B["accd"], self.ones_d], writes=[self.psd[bank]])
        self.rstd_from_bank(B, bank, rstd, rstd_d)

    def rms_rstd(self, B, src_chunks, src_deps, rstd, rstd_d):
        for c in range(KC):
            self.sumsq_acc(B, c, src_chunks[c], src_deps[c])
        self.sumsq_finish(B, 6, rstd, rstd_d)

    def rstd_from_bank(self, B, bank, rstd, rstd_d):
        P = self.P
        tmp, tmpd = B["rt"], B["rtd"]
        P.op(P.act, lambda h: h.activation(out=tmp[:], in_=self.psum[bank][:], func=AF.Sqrt,
                                           bias=self.epsc[:], scale=1.0 / D),
             reads=[self.psd[bank], self.ones_d], writes=[tmpd])
        P.op(P.dve, lambda h: h.reciprocal(out=rstd[:], in_=tmp[:]), reads=[tmpd], writes=[rstd_d])

    def ffn_bufs(self, es):
        B = {}
        B["x"] = self.sb("f_x", [128, KC, TT], F32, es)
        B["xd"] = deps(KC, "x")
        B["o"] = self.sb("f_o", [128, KC, TT], BF16, es)
        B["od"] = deps(KC, "o")
        B["hx"] = self.sb("f_hx", [128, KC, TT], BF16, es)
        B["hxd"] = deps(KC, "hx")
        B["sqf"] = self.sb("f_sqf", [128, 2, TT], F32, es)
        B["sqfd"] = deps(2, "sqf")
        B["acc"] = self.sb("f_acc", [128, TT], F32, es)
        B["accd"] = Dep("acc")
        B["hid"] = self.sb("f_hid", [128, FC, TT], BF16, es)
        B["hidd"] = deps(FC, "hid")
        B["ts"] = self.sb("f_ts", [128, 2, TT], BF16, es)
        B["tsd"] = deps(2, "ts")
        B["xn"] = self.sb("f_xn", [128, 2, TT], F32, es)
        B["xnd"] = deps(2, "xn")
        B["rt"] = self.sb("f_rt", [128, TT], F32, es)
        B["rtd"] = Dep("rt")
        B["rstd"] = self.sb("f_rstd", [128, TT], F32, es)
        B["rstdd"] = Dep("rstd")
        B["rstd2"] = self.sb("f_rstd2", [128, TT], F32, es)
        B["rstd2d"] = Dep("rstd2")
        NWB = 4
        B["win"] = self.sb("f_win", [128, NWB, D], BF16, es)
        B["wind"] = deps(NWB, "win")
        B["wout"] = self.sb("f_wout", [128, 2, F], BF16, es)
        B["woutd"] = deps(2, "wout")
        return B

    def load_x(self, B, t, src=None, src_dep=None):
        P = self.P
        c0 = self.tile_cols(t)
        src = self.xs if src is None else src
        sd = self.xs_dep[t] if src_dep is None else src_dep
        for h in range(2):
            ks = slice(h * 8, h * 8 + 8)
            P.dma(P.sp if h == 0 else P.act, B["x"][:, ks, :],
                  src[h * 1024:(h + 1) * 1024, c0:c0 + TT].rearrange("(k p) n -> p k n", p=128),
                  reads=[sd], writes=B["xd"][h * 8:h * 8 + 8])

    def store_x(self, B, t, dst=None, dst_dep=None, is_output=False, ncols=TT):
        P = self.P
        c0 = self.tile_cols(t)
        dst = self.xs if dst is None else dst
        dd = self.xs_dep[t] if dst_dep is None else dst_dep
        for h in range(2):
            ks = slice(h * 8, h * 8 + 8)
            P.dma(P.sp, dst[h * 1024:(h + 1) * 1024, c0:c0 + ncols].rearrange("(k p) n -> p k n", p=128),
                  B["x"][:, ks, 0:ncols], reads=B["xd"][h * 8:h * 8 + 8], writes=[dd], is_output=is_output)

    def pre_norm(self, B, t, idx):
        P = self.P
        x, xd = B["x"], B["xd"]
        self.rms_rstd(B, [x[:, c, :] for c in range(KC)], xd, B["rstd"], B["rstdd"])
        for c in range(KC):
            i = c % 2
            P.op(P.dve, lambda h, c=c, i=i: h.tensor_tensor(out=B["xn"][:, i, :], in0=x[:, c, :], in1=B["rstd"][:],
                                                           op=ALU.mult),
                 reads=[xd[c], B["rstdd"]], writes=[B["xnd"][i]])
            for (a, b, r) in self.segs(t):
                P.op(P.act, lambda h, c=c, i=i, a=a, b=b, r=r: h.activation(
                    out=B["hx"][:, c, a:b], in_=B["xn"][:, i, a:b], func=AF.Identity,
                    bias=self.modB[:, idx, c, r:r + 1], scale=self.modA[:, idx, c, r:r + 1]),
                     reads=[B["xnd"][i], self.mod_d], writes=[B["hxd"][c]])

    def post_update(self, B, t, idx):
        P = self.P
        x, xd = B["x"], B["xd"]
        for c in range(KC):
            i = c % 2
            P.op(P.dve, lambda h, c=c, i=i: h.tensor_tensor(out=B["xn"][:, i, :], in0=B["o"][:, c, :],
                                                           in1=B["rstd2"][:], op=ALU.mult),
                 reads=[B["od"][c], B["rstd2d"]], writes=[B["xnd"][i]])
            for (a, b, r) in self.segs(t):
                P.op(P.dve, lambda h, c=c, i=i, a=a, b=b, r=r: h.scalar_tensor_tensor(
                    out=x[:, c, a:b], in0=B["xn"][:, i, a:b], scalar=self.modG[:, idx, c, r:r + 1],
                    in1=x[:, c, a:b], op0=ALU.mult, op1=ALU.add),
                     reads=[B["xnd"][i], self.mod_d, xd[c]], writes=[xd[c]])

    def ffn_core(self, B, l, s):
        P = self.P
        win_b, wout_b = self.fwin_b[l][s], self.fwout_b[l][s]
        NWB = len(B["wind"])
        wi = 0
        for j in range(FC):
            bi = []
            for which in range(2):
                b = wi % NWB
                wi += 1
                jj = j + which * FC
                P.dma(P.sp, B["win"][:, b, :], win_b[jj], reads=[self.fwin_dep[l][s]], writes=[B["wind"][b]])
                bi.append(b)
            pg, pu = (2 * (j % 3)), (2 * (j % 3) + 1)
            for which, pb in ((0, pg), (1, pu)):
                b = bi[which]
                P.mm_group(self.psum[pb][:],
                           [(B["win"][:, b, k * 128:(k + 1) * 128], B["hx"][:, k, :]) for k in range(KC)],
                           reads=[B["wind"][b]] + B["hxd"], writes=[self.psd[pb]])
            ti = j % 2
            P.op(P.act, lambda h, pg=pg, ti=ti: h.activation(out=B["ts"][:, ti, :], in_=self.psum[pg][:], func=AF.Silu),
                 reads=[self.psd[pg]], writes=[B["tsd"][ti]])
            P.op(P.dve, lambda h, pu=pu, ti=ti, j=j: h.tensor_tensor(out=B["hid"][:, j, :], in0=B["ts"][:, ti, :],
                                                                    in1=self.psum[pu][:], op=ALU.mult),
                 reads=[B["tsd"][ti], self.psd[pu]], writes=[B["hidd"][j]])
        for c in range(KC):
            b = c % 2
            P.dma(P.sp, B["wout"][:, b, :], wout_b[c], reads=[self.fwout_dep[l][s]], writes=[B["woutd"][b]])
            pb = c % 6
            P.mm_group(self.psum[pb][:],
                       [(B["wout"][:, b, j * 128:(j + 1) * 128], B["hid"][:, j, :]) for j in range(FC)],
                       reads=[B["woutd"][b]] + B["hidd"], writes=[self.psd[pb]])
            self.sumsq_acc(B, c, self.psum[pb][:], self.psd[pb])
            P.op(P.dve, lambda h, pb=pb, c=c: h.tensor_copy(out=B["o"][:, c, :], in_=self.psum[pb][:]),
                 reads=[self.psd[pb]], writes=[B["od"][c]])
        self.sumsq_finish(B, 7, B["rstd2"], B["rstd2d"])

    def ffn_sublayer(self, B, t, l, s):
        idx = l * 3 + (0 if s == 0 else 2)
        self.pre_norm(B, t, idx)
        self.ffn_core(B, l, s)
        self.post_update(B, t, idx)

    def sweep_ffn(self, l, s, tiles, src=None, final=False):
        P = self.P
        with contextlib.ExitStack() as es:
            B = self.ffn_bufs(es)
            for t in tiles:
                if src is not None:
                    self.load_x(B, t, src=src, src_dep=Dep())
                else:
                    self.load_x(B, t)
                self.ffn_sublayer(B, t, l, s)
                if final:
                    self.store_x(B, t, dst=self.y_out, dst_dep=Dep(), is_output=True)
                else:
                    self.store_x(B, t)
            P.barrier()

    def sweep_ffn_u(self, tiles):
        P = self.P
        l = 1
        with contextlib.ExitStack() as es:
            B = self.ffn_bufs(es)
            ub = self.sb("u_buf", [128, 2, TT], F32, es)
            ubd = deps(2)
            for t in tiles:
                c0 = self.tile_cols(t)
                self.load_x(B, t)
                self.ffn_sublayer(B, t, l, 0)
                self.store_x(B, t, dst=self.x4_out, dst_dep=Dep(), is_output=True)
                self.pre_norm(B, t, l * 3 + 1)
                for c in range(KC):
                    b = c % 4
                    P.dma(P.sp, B["win"][:, b, :], self.suw_b[c], reads=[self.suw_dep], writes=[B["wind"][b]])
                    pb = c % 4
                    P.mm_group(self.psum[pb][:],
                               [(B["win"][:, b, k * 128:(k + 1) * 128], B["hx"][:, k, :]) for k in range(KC)],
                               reads=[B["wind"][b]] + B["hxd"], writes=[self.psd[pb]])
                    i = c % 2
                    P.op(P.act, lambda h, pb=pb, i=i: h.activation(out=ub[:, i, :], in_=self.psum[pb][:], func=AF.Copy),
                         reads=[self.psd[pb]], writes=[ubd[i]])
                    P.dma(P.sp, self.u_out[c * 128:(c + 1) * 128, c0:c0 + TT], ub[:, i, :], reads=[ubd[i]], writes=[Dep()],
                          is_output=True)
            P.barrier()

    def sweep_glu_ffn(self, tiles):
        P = self.P
        l = 1
        with contextlib.ExitStack() as es:
            B = self.ffn_bufs(es)
            gf = self.sb("g_f", [128, 2, TT], F32, es)
            gfd = deps(2)
            for t in tiles:
                c0 = self.tile_cols(t)
                self.load_x(B, t, src=self.x_in, src_dep=Dep())
                for c in range(KC):
                    i = c % 2
                    P.dma(P.act, gf[:, i, :], self.g_in[c * 128:(c + 1) * 128, c0:c0 + TT], writes=[gfd[i]])
                    P.op(P.act, lambda h, c=c, i=i: h.activation(out=B["hx"][:, c, :], in_=gf[:, i, :], func=AF.Copy),
                         reads=[gfd[i]], writes=[B["hxd"][c]])
                for c in range(KC):
                    bs = []
                    for which in range(2):
                        b = (2 * c + which) % 4
                        P.dma(P.sp, B["win"][:, b, :], self.gw_b[c + which * KC], reads=[self.gw_dep], writes=[B["wind"][b]])
                        bs.append(b)
                    pv, pg = 2 * (c % 3), 2 * (c % 3) + 1
                    for which, pb in ((0, pv), (1, pg)):
                        b = bs[which]
                        P.mm_group(self.psum[pb][:],
                                   [(B["win"][:, b, k * 128:(k + 1) * 128], B["hx"][:, k, :]) for k in range(KC)],
                                   reads=[B["wind"][b]] + B["hxd"], writes=[self.psd[pb]])
                    ti = c % 2
                    P.op(P.act, lambda h, pg=pg, ti=ti: h.activation(out=B["xn"][:, ti, :], in_=self.psum[pg][:], func=AF.Sigmoid),
                         reads=[self.psd[pg]], writes=[B["xnd"][ti]])
                    P.op(P.dve, lambda h, pv=pv, ti=ti: h.tensor_tensor(out=B["xn"][:, ti, :], in0=B["xn"][:, ti, :],
                                                                     in1=self.psum[pv][:], op=ALU.mult),
                         reads=[B["xnd"][ti], self.psd[pv]], writes=[B["xnd"][ti]])
                    self.sumsq_acc(B, c, B["xn"][:, ti, :], B["xnd"][ti])
                    P.op(P.dve, lambda h, c=c, ti=ti: h.tensor_copy(out=B["o"][:, c, :], in_=B["xn"][:, ti, :]),
                         reads=[B["xnd"][ti]], writes=[B["od"][c]])
                self.sumsq_finish(B, 7, B["rstd2"], B["rstd2d"])
                self.post_update(B, t, l * 3 + 1)
                self.ffn_sublayer(B, t, l, 1)
                self.store_x(B, t, dst=self.y_out, dst_dep=Dep(), is_output=True)
            P.barrier()

    def attn_decl(self):
        nt = self.ntok
        self.aqw_in = self.din("aqw", [32, 128, D])
        self.akw_in = self.din("akw", [8, 128, D])
        self.avw_in = self.din("avw", [128, KC * 512])
        self.aow_in = self.din("aow", [KC, 128, D])
        self.aqw_b = self.dscr("aqwb", [32, 128, D], BF16)
        self.akw_b = self.dscr("akwb", [8, 128, D], BF16)
        self.avw_b = self.dscr("avwb", [128, KC * 512], BF16)
        self.aow_b = self.dscr("aowb", [KC, 128, D], BF16)
        self.aw_dep = Dep("aw")
        self.rope_in = self.din("rope", [4, 128, nt])
        self.mask_in = self.din("amask", [128, 4, 384])
        self.sink_in = self.din("asink", [128, N_HEADS])
        self.ident_in = self.din("ident", [128, 128])
        self.kT_s = self.dscr("kTs", [N_KV, 128, nt], BF16)
        self.v_s = self.dscr("vs", [nt // 128, 128, 512], BF16)
        self.kv_dep = deps(self.ntile, "kv")

    def attn_convert(self):
        self.convert(self.aqw_in, self.aqw_b, self.aw_dep)
        self.convert(self.akw_in, self.akw_b, self.aw_dep)
        P = self.P
        for k0 in range(0, KC, 4):
            P.dma(P.pool, self.avw_b[:, k0 * 512:(k0 + 4) * 512], self.avw_in[:, k0 * 512:(k0 + 4) * 512],
                  writes=[self.aw_dep], max_dma_last_dim=8192)
        self.convert(self.aow_in, self.aow_b, self.aw_dep)

    def kv_bufs(self, es, B):
        A = {}
        A["cs"] = self.sb("a_cs", [128, 2, TT], F32, es)
        A["csd"] = Dep("cs")
        A["kb"] = self.sb("a_kb", [128, N_KV, TT], BF16, es)
        A["kbd"] = Dep("kb")
        A["vb"] = self.sb("a_vb", [128, 4, 512], BF16, es)
        A["vbd"] = Dep("vb")
        A["t1"] = B["xn"]
        A["t1d"] = B["xnd"]
        A["wk"] = B["win"]
        A["wkd"] = B["wind"]
        A["wv"] = B["wout"][:].rearrange("p a f -> p (a f)")[:, 0:KC * 512]
        A["wvd"] = list(B["woutd"])
        return A

    def rope_evac(self, A, pa, pb, out_ap, out_dep, cos_ap, sin_ap):
        P = self.P
        i0, i1 = 0, 1
        P.op(P.dve, lambda h: h.tensor_tensor(out=A["t1"][:, i0, :], in0=self.psum[pb][:], in1=sin_ap, op=ALU.mult),
             reads=[self.psd[pb], A["csd"]], writes=[A["t1d"][i0]])
        P.op(P.dve, lambda h: h.tensor_tensor(out=A["t1"][:, i1, :], in0=self.psum[pa][:], in1=cos_ap, op=ALU.mult),
             reads=[self.psd[pa], A["csd"]], writes=[A["t1d"][i1]])
        P.op(P.dve, lambda h: h.tensor_tensor(out=out_ap, in0=A["t1"][:, i0, :], in1=A["t1"][:, i1, :], op=ALU.add),
             reads=A["t1d"], writes=[out_dep])

    def kv_project(self, B, A, t):
        P = self.P
        c0 = self.tile_cols(t)
        P.dma(P.act, A["cs"][:], self.rope_in[0:2, :, c0:c0 + TT].rearrange("a p n -> p a n"), writes=[A["csd"]])
        for g in range(N_KV):
            bs = []
            for which in range(2):
                b = (2 * g + which) % 4
                P.dma(P.sp, A["wk"][:, b, :], self.akw_b[g + which * N_KV], reads=[self.aw_dep], writes=[A["wkd"][b]])
                bs.append(b)
            pa, pb = 2 * (g % 2), 2 * (g % 2) + 1
            for which, pbk in ((0, pa), (1, pb)):
                b = bs[which]
                P.mm_group(self.psum[pbk][:],
                           [(A["wk"][:, b, k * 128:(k + 1) * 128], B["hx"][:, k, :]) for k in range(KC)],
                           reads=[A["wkd"][b]] + B["hxd"], writes=[self.psd[pbk]])
            self.rope_evac(A, pa, pb, A["kb"][:, g, :], A["kbd"], A["cs"][:, 0, :], A["cs"][:, 1, :])
        P.dma(P.sp, self.kT_s[:, :, c0:c0 + TT].rearrange("g p n -> p g n"), A["kb"][:], reads=[A["kbd"]],
              writes=[self.kv_dep[t]])
        for blk in range(4):
            pbk = 4 + blk % 2
            P.mm_group(self.psum[pbk][:],
                       [(B["hx"][:, k, blk * 128:(blk + 1) * 128], A["wv"][:, k * 512:(k + 1) * 512]) for k in range(KC)],
                       reads=A["wvd"] + B["hxd"], writes=[self.psd[pbk]])
            P.op(P.act, lambda h, pbk=pbk, blk=blk: h.activation(out=A["vb"][:, blk, :], in_=self.psum[pbk][:], func=AF.Copy),
                 reads=[self.psd[pbk]], writes=[A["vbd"]])
        P.dma(P.sp, self.v_s[c0 // 128:c0 // 128 + 4].rearrange("b p n -> p b n"), A["vb"][:], reads=[A["vbd"]],
              writes=[self.kv_dep[t]])

    def sweep_ffn_kv(self, l, s, tiles, src=None):
        P = self.P
        with contextlib.ExitStack() as es:
            B = self.ffn_bufs(es)
            A = self.kv_bufs(es, B)
            for t in tiles:
                if src is not None:
                    self.load_x(B, t, src=src, src_dep=Dep())
                else:
                    self.load_x(B, t)
                if not self.skip_ffn:
                    self.ffn_sublayer(B, t, l, s)
                self.store_x(B, t)
                self.pre_norm(B, t, l * 3 + 1)
                P.dma(P.act, A["wv"], self.avw_b, reads=[self.aw_dep], writes=A["wvd"])
                self.kv_project(B, A, t)
            P.barrier()

    def win_cols(self, t):
        tpc = self.tpc
        out = []
        for i in range(6):
            tb = t * TT - 128 + 128 * i
            if tb < 0:
                src = tpc
            elif tb >= tpc:
                src = tpc + 128
            else:
                src = tb
            out.append((128 * i, src, 128))
        return out

    def attn_bufs(self, es):
        A = {}
        A["cs"] = self.sb("b_cs", [128, 2, TT], F32, es)
        A["csd"] = Dep()
        A["t1"] = self.sb("b_t1", [128, 2, TT], F32, es)
        A["t1d"] = deps(2)
        A["q"] = self.sb("b_q", [128, N_HEADS, TT], BF16, es)
        A["qd"] = deps(N_HEADS)
        A["ao"] = self.sb("b_ao", [128, N_HEADS, TT], BF16, es)
        A["aod"] = deps(N_HEADS)
        A["kw"] = self.sb("b_kw", [128, N_KV, 768], BF16, es)
        A["kwd"] = Dep()
        A["vw"] = self.sb("b_vw", [128, 6, 512], BF16, es)
        A["vwd"] = Dep()
        A["kc"] = self.sb("b_kc", [128, N_KV, CTX], BF16, es)
        A["vc"] = self.sb("b_vc", [128, 2, 512], BF16, es)
        A["kcd"] = Dep()
        A["mask"] = self.sb("b_mask", [128, 4, 384], F32, es)
        A["sink"] = self.sb("b_sink", [128, N_HEADS], F32, es)
        A["cd"] = Dep()
        A["ident"] = self.sb("b_ident", [128, 128], BF16, es)
        A["s"] = self.sb("b_s", [128, 4, 640], F32, es)
        A["sd"] = deps(4)
        A["e"] = self.sb("b_e", [128, 4, 640], BF16, es)
        A["ed"] = deps(4)
        A["en"] = self.sb("b_en", [128, 4, 640], BF16, es)
        A["end"] = deps(4)
        A["pT"] = self.sb("b_pT", [128, 4, 640], BF16, es)
        A["pTd"] = deps(4)
        A["st"] = self.sb("b_st", [128, 4, 8], F32, es)
        A["std"] = deps(4)
        A["wq"] = self.sb("b_wq", [128, 4, D], BF16, es)
        A["wqd"] = deps(4)
        return A

    def attn_tile(self, B, A, t, qblocks):
        P = self.P
        ident = A["ident"]
        units = [(qb, mk, h) for (qb, mk) in qblocks for h in range(N_HEADS)]
        ND = len(A["sd"])

        def st1(u, qb, mk, h):
            g = h // 4
            i = u % ND
            qs = slice(qb * 128, qb * 128 + 128)
            pbA, pbB = 2 + (u % 2), 4 + (u % 2)
            P.mm_group(self.psum[pbA][:, 0:384], [(A["q"][:, h, qs], A["kw"][:, g, qb * 128:qb * 128 + 384])],
                       reads=[A["qd"][h], A["kwd"]], writes=[self.psd[pbA]])
            P.mm_group(self.psum[pbB][:, 0:CTX], [(A["q"][:, h, qs], A["kc"][:, g, :])],
                       reads=[A["qd"][h], A["kcd"]], writes=[self.psd[pbB]])
            S = A["s"][:, i, :]
            P.op(P.dve, lambda hh: hh.tensor_tensor(
                out=S[:, 0:384], in0=self.psum[pbA][:, 0:384], in1=A["mask"][:, mk, :], op=ALU.add),
                 reads=[self.psd[pbA], A["cd"]], writes=[A["sd"][i]])
            P.op(P.act, lambda hh: hh.activation(out=S[:, 384:640], in_=self.psum[pbB][:, 0:CTX], func=AF.Copy),
                 reads=[self.psd[pbB]], writes=[A["sd"][i]])
            st = A["st"][:, i, :]
            P.op(P.dve, lambda hh: hh.reduce_max(out=st[:, 0:1], in_=S, axis=mybir.AxisListType.X),
                 reads=[A["sd"][i]], writes=[A["std"][i]])
            P.op(P.dve, lambda hh: hh.tensor_tensor(out=st[:, 0:1], in0=st[:, 0:1], in1=A["sink"][:, h:h + 1], op=ALU.max),
                 reads=[A["std"][i], A["cd"]], writes=[A["std"][i]])
            P.op(P.dve, lambda hh: hh.tensor_scalar(st[:, 1:2], st[:, 0:1], -1.0, None, ALU.mult),
                 reads=[A["std"][i]], writes=[A["std"][i]])

        def st2(u, qb, mk, h):
            i = u % ND
            S = A["s"][:, i, :]
            st = A["st"][:, i, :]
            E = A["e"][:, i, :]
            P.op(P.act, lambda hh: hh.activation(out=E, in_=S, func=AF.Exp, bias=st[:, 1:2], scale=1.0,
                                                 accum_out=st[:, 2:3]),
                 reads=[A["sd"][i], A["std"][i]], writes=[A["ed"][i], A["std"][i]])
            P.op(P.act, lambda hh: hh.activation(out=st[:, 3:4], in_=A["sink"][:, h:h + 1], func=AF.Exp,
                                                 bias=st[:, 1:2], scale=1.0),
                 reads=[A["std"][i], A["cd"]], writes=[A["std"][i]])
            P.op(P.dve, lambda hh: hh.tensor_tensor(out=st[:, 4:5], in0=st[:, 2:3], in1=st[:, 3:4], op=ALU.add),
                 reads=[A["std"][i]], writes=[A["std"][i]])
            P.op(P.dve, lambda hh: hh.reciprocal(out=st[:, 5:6], in_=st[:, 4:5]),
                 reads=[A["std"][i]], writes=[A["std"][i]])
            EN = A["en"][:, i, :]
            P.op(P.dve, lambda hh: hh.tensor_scalar(EN, E, st[:, 5:6], None, ALU.mult),
                 reads=[A["ed"][i], A["std"][i]], writes=[A["end"][i]])

        def st3(u, qb, mk, h):
            i = u % ND
            EN = A["en"][:, i, :]
            pt_ps = self.psum[6][:].bitcast(BF16)
            P.sync_for(P.pe, [A["end"][i], A["cd"]], [self.psd[6]])
            inst = None
            for blk in range(5):
                inst = P.pe.h.transpose(pt_ps[:, blk * 128:(blk + 1) * 128], EN[:, blk * 128:(blk + 1) * 128], ident[:])
            P.pe.cnt += 1
            inst.then_inc(P.pe.sem, 1)
            ev = Ev(P.pe.sem, P.pe.cnt, "e_pe")
            P.commit(ev, [A["end"][i], A["cd"]], [self.psd[6]])
            PT = A["pT"][:, i, :]
            P.op(P.act, lambda hh: hh.activation(out=PT, in_=pt_ps[:, 0:640], func=AF.Copy),
                 reads=[self.psd[6]], writes=[A["pTd"][i]])

        def st4(u, qb, mk, h):
            g = h // 4
            i = u % ND
            qs = slice(qb * 128, qb * 128 + 128)
            PT = A["pT"][:, i, :]
            pairs = []
            for blk in range(3):
                pairs.append((A["vw"][:, qb + blk, g * 128:(g + 1) * 128], PT[:, blk * 128:(blk + 1) * 128]))
            for blk in range(2):
                pairs.append((A["vc"][:, blk, g * 128:(g + 1) * 128], PT[:, (3 + blk) * 128:(4 + blk) * 128]))
            P.mm_group(self.psum[7][:, 0:128], pairs, reads=[A["vwd"], A["kcd"], A["pTd"][i]], writes=[self.psd[7]])
            P.op(P.act, lambda hh: hh.activation(out=A["ao"][:, h, qs], in_=self.psum[7][:, 0:128], func=AF.Copy),
                 reads=[self.psd[7]], writes=[A["aod"][h]])

        stages = [st1, st2, st3, st4]
        if self.trunc < 99:
            stages = stages[:max(1, min(4, self.trunc - 1))]
        n = len(units)
        for step in range(n + len(stages) - 1):
            for si, fn in enumerate(stages):
                u = step - si
                if 0 <= u < n:
                    fn(u, *units[u])

    def sweep_attn(self, tiles):
        P = self.P
        l = 0
        tpc = self.tpc
        with contextlib.ExitStack() as es:
            B = self.ffn_bufs_small(es)
            A = self.attn_bufs(es)
            P.dma(P.act, A["mask"][:], self.mask_in, writes=[A["cd"]])
            P.dma(P.act, A["sink"][:], self.sink_in, writes=[A["cd"]])
            P.dma(P.pool, A["ident"][:], self.ident_in, writes=[A["cd"]])
            cc = tpc + 256
            P.dma(P.act, A["kc"][:], self.kT_s[:, :, cc:cc + CTX].rearrange("g p n -> p g n"),
                  reads=[self.kv_dep[self.nt_own]], writes=[A["kcd"]])
            P.dma(P.act, A["vc"][:], self.v_s[cc // 128:cc // 128 + 2].rearrange("b p n -> p b n"),
                  reads=[self.kv_dep[self.nt_own]], writes=[A["kcd"]])
            for t in tiles:
                c0 = self.tile_cols(t)
                own = t < self.nt_own
                self.load_x(B, t)
                self.pre_norm(B, t, l * 3 + 1)
                P.dma(P.act, A["cs"][:], self.rope_in[2:4, :, c0:c0 + TT].rearrange("a p n -> p a n"), writes=[A["csd"]])
                if own:
                    kvr = self.kv_dep
                    for (doff, scol, n) in self.win_cols(t):
                        P.dma(P.act, A["kw"][:, :, doff:doff + n], self.kT_s[:, :, scol:scol + n].rearrange("g p n -> p g n"),
                              reads=kvr, writes=[A["kwd"]])
                        P.dma(P.act, A["vw"][:, doff // 128:(doff + n) // 128, :],
                              self.v_s[scol // 128:(scol + n) // 128].rearrange("b p n -> p b n"), reads=kvr, writes=[A["vwd"]])
                for h in range(N_HEADS):
                    bs = []
                    for which in range(2):
                        b = (2 * h + which) % 4
                        P.dma(P.sp, A["wq"][:, b, :], self.aqw_b[h + which * N_HEADS], reads=[self.aw_dep], writes=[A["wqd"][b]])
                        bs.append(b)
                    pa, pb = 0, 1
                    for which, pbk in ((0, pa), (1, pb)):
                        b = bs[which]
                        P.mm_group(self.psum[pbk][:],
                                   [(A["wq"][:, b, k * 128:(k + 1) * 128], B["hx"][:, k, :]) for k in range(KC)],
                                   reads=[A["wqd"][b]] + B["hxd"], writes=[self.psd[pbk]])
                    self.rope_evac(A, pa, pb, A["q"][:, h, :], A["qd"][h], A["cs"][:, 0, :], A["cs"][:, 1, :])
                if self.trunc <= 1:
                    continue
                if own:
                    qbl = []
                    for qb in range(4):
                        mk = 0
                        if t == 0 and qb == 0:
                            mk = 1
                        if t == self.nt_own - 1 and qb == 3:
                            mk = 2
                        qbl.append((qb, mk))
                else:
                    qbl = [(2, 3), (3, 3)]
                    for hh in range(N_HEADS):
                        pass
                self.attn_tile(B, A, t, qbl)
                if self.trunc <= 4:
                    continue
                nq = [q for (q, _) in qbl]
                a_, b_ = nq[0] * 128, nq[-1] * 128 + 128
                if not own:
                    for h in range(N_HEADS):
                        P.op(P.pool, lambda hh, h=h: hh.memset(A["ao"][:, h, 0:256], 0.0), reads=[], writes=[A["aod"][h]])
                for c in range(KC):
                    b = c % 4
                    P.dma(P.sp, A["wq"][:, b, :], self.aow_b[c], reads=[self.aw_dep], writes=[A["wqd"][b]])
                    pb = c % 2
                    P.mm_group(self.psum[pb][:],
                               [(A["wq"][:, b, k * 128:(k + 1) * 128], A["ao"][:, k, :]) for k in range(KC)],
                               reads=[A["wqd"][b]] + A["aod"], writes=[self.psd[pb]])
                    self.sumsq_acc(B, c, self.psum[pb][:], self.psd[pb])
                    P.op(P.dve, lambda h, pb=pb, c=c: h.tensor_copy(out=B["o"][:, c, :], in_=self.psum[pb][:]),
                         reads=[self.psd[pb]], writes=[B["od"][c]])
                if self.trunc <= 5:
                    continue
                self.sumsq_finish(B, 7, B["rstd2"], B["rstd2d"])
                if self.trunc <= 6:
                    continue
                self.post_update(B, t, l * 3 + 1)
                if self.trunc <= 7:
                    continue
                self.store_x(B, t)
            P.barrier()

    def ffn_bufs_small(self, es):
        B = {}
        B["x"] = self.sb("s_x", [128, KC, TT], F32, es)
        B["xd"] = deps(KC, "x")
        B["o"] = self.sb("s_o", [128, KC, TT], BF16, es)
        B["od"] = deps(KC, "o")
        B["hx"] = self.sb("s_hx", [128, KC, TT], BF16, es)
        B["hxd"] = deps(KC, "hx")
        B["sqf"] = self.sb("s_sqf", [128, 2, TT], F32, es)
        B["sqfd"] = deps(2, "sqf")
        B["acc"] = self.sb("s_acc", [128, TT], F32, es)
        B["accd"] = Dep("acc")
        B["xn"] = self.sb("s_xn", [128, 2, TT], F32, es)
        B["xnd"] = deps(2, "xn")
        B["rt"] = self.sb("s_rt", [128, TT], F32, es)
        B["rtd"] = Dep("rt")
        B["rstd"] = self.sb("s_rstd", [128, TT], F32, es)
        B["rstdd"] = Dep("rstd")
        B["rstd2"] = self.sb("s_rstd2", [128, TT], F32, es)
        B["rstd2d"] = Dep("rstd2")
        return B

    def run_phases(self):
        if "launchA" in self.phases:
            self.sweep_ffn_kv(0, 0, list(range(self.ntile)), src=self.x_in)
            self.sweep_attn(list(range(self.ntile)))
            self.sweep_ffn(0, 1, list(range(self.ntile)))
            self.sweep_ffn_u(list(range(self.ntile)))
            return
        if "launchC" in self.phases:
            self.sweep_glu_ffn(list(range(self.ntile)))
            return
        if "modc" in self.phases:
            return
        if "t_ffn00" in self.phases:
            self.sweep_ffn(0, 0, list(range(self.nt_own)), src=self.x_in, final=True)
            return
        if "t_copy" in self.phases:
            P = self.P
            with contextlib.ExitStack() as es:
                B = self.ffn_bufs_small(es)
                for t in range(self.nt_own):
                    self.load_x(B, t, src=self.x_in, src_dep=Dep())
                    if "t_norm" in self.phases:
                        self.pre_norm(B, t, 1)
                        for c in range(KC):
                            P.op(P.dve, lambda h, c=c: h.tensor_copy(out=B["x"][:, c, :], in_=B["hx"][:, c, :]),
                                 reads=[B["hxd"][c]], writes=[B["xd"][c]])
                    self.store_x(B, t, dst=self.y_out, dst_dep=Dep(), is_output=True)
            return
        if "t_attn" in self.phases:
            self.sweep_ffn_kv(0, 0, list(range(self.ntile)), src=self.x_in)
            if "t_noattn" not in self.phases:
                self.sweep_attn(list(range(self.ntile)))
            self.sweep_copy_out()
            return
        raise NotImplementedError

    def sweep_copy_out(self):
        P = self.P
        with contextlib.ExitStack() as es:
            B = {"x": self.sb("c_x", [128, KC, TT], F32, es), "xd": deps(KC)}
            for t in range(self.nt_own):
                self.load_x(B, t)
                self.store_x(B, t, dst=self.y_out, dst_dep=Dep(), is_output=True)
            P.barrier()


def host_common(inputs, tpc, core, ncore_per_batch=4):
    b = core // ncore_per_batch
    j = core % ncore_per_batch
    x = inputs["x"]
    seq = x.shape[1]
    t0 = j * tpc
    own = x[b, t0:t0 + tpc]
    hl = x[b, t0 - 128:t0] if t0 >= 128 else np.zeros((128, D), np.float32)
    hr = x[b, t0 + tpc:t0 + tpc + 128] if t0 + tpc + 128 <= seq else np.zeros((128, D), np.float32)
    ctx = inputs["ctx"][b]
    xin = np.ascontiguousarray(np.concatenate([own, hl, hr, ctx], axis=0).T)
    m = {"xin": xin}
    cm = np.stack([inputs["c"][b], inputs["c_ctx"]], axis=-1)
    m["cm"] = np.ascontiguousarray(cm.reshape(KC, 128, 2).transpose(1, 0, 2))
    return m


def host_shared(inputs):
    m = {}
    for l in range(2):
        m[f"ada{l}"] = tile_w(inputs["ada_w"][l])
        for s in range(2):
            m[f"fwin{l}{s}"] = tile_w(inputs["ffn_w_in"][l, s])
            m[f"fwout{l}{s}"] = tile_w(inputs["ffn_w_out"][l, s])
    ab = inputs["ada_b"]
    m["adab"] = np.ascontiguousarray(ab.reshape(2, N_MOD * KC, 128).transpose(2, 0, 1))
    ng = inputs["norm_g"].reshape(12, KC, 128)
    m["ng"] = np.ascontiguousarray(ng.transpose(2, 0, 1))
    return m


def rope_partner():
    p = np.arange(128)
    return np.where((p % 64) < 32, p + 32, p - 32)


def host_attn_shared(inputs):
    m = {}
    w = inputs["attn_w_in"][0]
    nq = N_HEADS * HEAD_DIM
    nk = N_KV * HEAD_DIM
    wq, wk, wv = w[:, :nq], w[:, nq:nq + nk], w[:, nq + nk:]
    part = rope_partner()
    qperm = (np.arange(N_HEADS)[:, None] * 128 + part[None, :]).reshape(-1)
    kperm = (np.arange(N_KV)[:, None] * 128 + part[None, :]).reshape(-1)
    m["aqw"] = np.concatenate([tile_w(wq), tile_w(wq[:, qperm])], axis=0)
    m["akw"] = np.concatenate([tile_w(wk), tile_w(wk[:, kperm])], axis=0)
    m["avw"] = np.ascontiguousarray(wv.reshape(KC, 128, 512).transpose(1, 0, 2)).reshape(128, KC * 512)
    m["aow"] = tile_w(inputs["attn_w_out"][0])
    m["asink"] = np.ascontiguousarray(np.broadcast_to(inputs["attn_sink"][0][None, :], (128, N_HEADS))).astype(np.float32)
    m["ident"] = np.eye(128, dtype=np.float32)
    return m


def host_attn_core(tpc, core, seq, ncore_per_batch=4):
    j = core % ncore_per_batch
    t0 = j * tpc
    pos = np.concatenate([np.arange(t0, t0 + tpc), np.arange(t0 - 128, t0), np.arange(t0 + tpc, t0 + tpc + 128)])
    pos = np.clip(pos, 0, seq - 1)
    row = (pos // GRID_W).astype(np.float32)
    col = (pos % GRID_W).astype(np.float32)
    inv = (10000.0 ** (-np.arange(0, 64, 2, dtype=np.float32) / 64)).astype(np.float32)
    p = np.arange(128)
    fidx = p % 32
    is_col = p >= 64
    sign = np.where((p % 64) < 32, -1.0, 1.0).astype(np.float32)
    ang = np.where(is_col[:, None], col[None, :] * inv[fidx][:, None], row[None, :] * inv[fidx][:, None]).astype(np.float32)
    cos = np.cos(ang).astype(np.float32)
    sin = (np.sin(ang) * sign[:, None]).astype(np.float32)
    cos = np.concatenate([cos, np.ones((128, CTX), np.float32)], axis=1)
    sin = np.concatenate([sin, np.zeros((128, CTX), np.float32)], axis=1)
    sc = np.float32(HEAD_DIM ** -0.5)
    rope = np.stack([cos, sin, cos * sc, sin * sc], axis=0).astype(np.float32)
    i = np.arange(128)[:, None]
    jj = np.arange(384)[None, :]
    rel = jj - i
    band = (rel >= 0) & (rel <= 256)
    NEG = np.float32(-1e30)
    def mk(valid):
        return np.where(valid, np.float32(0), NEG).astype(np.float32)
    m0 = mk(band)
    m1 = mk(band & (jj >= 128)) if j == 0 else m0
    m2 = mk(band & (jj < 256)) if j == ncore_per_batch - 1 else m0
    m3 = mk(np.zeros_like(band))
    amask = np.ascontiguousarray(np.stack([m0, m1, m2, m3], axis=1))
    return {"rope": rope, "amask": amask}


TWO_PI = 6.283185307179586
CW1 = 6.28125
CW2 = TWO_PI - 6.28125
MAGIC = 12582912.0
NB_SSM = 4


class SSMK:
    def __init__(self, lseq):
        self.lseq = lseq
        self.ntok = CTX + lseq
        self.nch = lseq // TT

    def build(self):
        nc = bass.Bass("TRN2", target_bir_lowering=False)
        self.nc = nc
        self.es = contextlib.ExitStack()
        es = self.es
        P = Prog(nc, es)
        self.P = P
        din = lambda n, s: nc.dram_tensor(n, list(s), F32, kind="ExternalInput").ap()
        self.u_in = din("uT", [NB_SSM * 128, self.ntok])
        self.pc_in = din("pc", [NB_SSM, 2, 128, 2 * 64 + 1 + 2 * 64])
        self.ps_in = din("pst", [NB_SSM, 2, 128, 3 * 8])
        self.cc_in = din("cc", [NB_SSM, 2, 2, 128, 128])
        self.dk_in = din("dk", [128, NB_SSM])
        self.k_in = din("kst", [128, 8 + 8 * 128 + 128 + TT])
        self.g_out = nc.dram_tensor("gT", [NB_SSM * 128, self.lseq], F32, kind="ExternalOutput").ap()
        sb = lambda n, s, dt=F32: es.enter_context(nc.sbuf_tensor(n, list(s), dt))
        self.psum = [es.enter_context(nc.psum_tensor(f"ps{i}", [128, TT], F32)) for i in range(8)]
        self.psd = deps(8, "ps")
        self.kst = sb("kst_sb", [128, 8 + 8 * 128 + 128 + TT])
        self.kd = Dep()
        P.dma(P.sp, self.kst[:], self.k_in, writes=[self.kd])
        self.rowmask = self.kst[:, 0:8]
        self.colmask = self.kst[:, 8:8 + 1024]
        self.swap = self.kst[:, 8 + 1024:8 + 1024 + 128]
        self.iota = self.kst[:, 8 + 1024 + 128:8 + 1024 + 128 + TT]
        self.dk = sb("dk_sb", [128, NB_SSM])
        P.dma(P.sp, self.dk[:], self.dk_in, writes=[self.kd])
        self.ybuf = sb("ybuf", [128, self.lseq])
        self.yd = deps(self.nch, "y")
        self.cos0 = sb("cos0", [128, 8, TT])
        self.sin0 = sb("sin0", [128, 8, TT])
        self.tabd = Dep("tab")
        self.z = sb("z", [128, 8, TT])
        self.zd = deps(8, "z")
        self.zinit = sb("zinit", [128, 8])
        self.zid = Dep("zinit")
        self.w = sb("w", [128, 2, TT])
        self.wd = deps(2, "w")
        self.t1 = sb("t1", [128, 2, TT])
        self.t1d = deps(2, "t1")
        self.Ab = sb("Ab", [128, 8, 2, TT], BF16)
        self.Abd = deps(8, "Ab")
        self.uf = sb("uf", [128, 2, TT])
        self.ufd = deps(2, "uf")
        self.ub = sb("ub", [128, 2, TT], BF16)
        self.ubd = deps(2, "ub")
        self.bp = sb("bp", [128, 2, 8, 128], BF16)
        self.cg = sb("cg", [128, 2, 8, 128], BF16)
        self.setd = Dep("set")
        self.st = sb("st", [128, 8, 8])
        self.sm = sb("sm", [128, 24, 64])
        self.sms = sb("sms", [128, 16, 8])
        self.pcb = sb("pcb", [128, 2 * 64 + 1 + 2 * 64])
        self.psb = sb("psb", [128, 24])
        self.ccb = sb("ccb", [128, 2, 128])
        self.ang = sb("ang", [128, 4, TT])
        self.gt = sb("gt", [128, 4, TT])
        self.gtd = Dep("gt")
        self.zs = sb("zs", [128, 3, 8])
        for bt in range(NB_SSM):
            for dr in range(2):
                self.setup(bt, dr)
                self.scan_dir(bt, dr)
            self.readout(bt)
        P.finish()
        es.close()
        return nc

    def dve(self, fn, reads, writes):
        return self.P.op(self.P.dve, fn, reads=reads, writes=writes)

    def reduce_angle(self, x, out, k, deps_):
        self.dve(lambda h: h.tensor_scalar(k, x, 1.0 / TWO_PI, MAGIC, ALU.mult, ALU.add), deps_, deps_)
        self.dve(lambda h: h.tensor_scalar(k, k, MAGIC, None, ALU.subtract), deps_, deps_)
        self.dve(lambda h: h.scalar_tensor_tensor(out, k, -CW1, x, ALU.mult, ALU.add), deps_, deps_)
        self.dve(lambda h: h.scalar_tensor_tensor(out, k, -CW2, out, ALU.mult, ALU.add), deps_, deps_)

    def sincos(self, x, s_out, c_out, tmp, k, d):
        P = self.P
        self.reduce_angle(x, tmp, k, [d])
        P.op(P.act, lambda h: h.activation(out=s_out, in_=tmp, func=AF.Sin), reads=[d], writes=[d])
        self.dve(lambda h: h.tensor_scalar(tmp, x, TWO_PI / 4, None, ALU.add), [d], [d])
        self.reduce_angle(tmp, tmp, k, [d])
        P.op(P.act, lambda h: h.activation(out=c_out, in_=tmp, func=AF.Sin), reads=[d], writes=[d])

    def setup(self, bt, dr):
        P = self.P
        d = self.setd
        sm, sms = self.sm, self.sms
        S = lambda i: sm[:, i, :]
        P.dma(P.sp, self.pcb[:], self.pc_in[bt, dr], writes=[d])
        P.dma(P.sp, self.psb[:], self.ps_in[bt, dr], writes=[d])
        P.dma(P.sp, self.ccb[:], self.cc_in[bt, dr].rearrange("a p n -> p a n"), writes=[d])
        a_re, a_im, ldt = self.pcb[:, 0:64], self.pcb[:, 64:128], self.pcb[:, 128:129]
        b_re, b_im = self.pcb[:, 129:193], self.pcb[:, 193:257]
        dt = sms[:, 0, 0:1]
        P.op(P.act, lambda h: h.activation(out=dt, in_=ldt, func=AF.Exp), reads=[d], writes=[d])
        self.dve(lambda h: h.tensor_scalar(S(0), a_re, dt, None, ALU.mult), [d], [d])
        P.op(P.act, lambda h: h.activation(out=S(1), in_=S(0), func=AF.Exp), reads=[d], writes=[d])
        self.dve(lambda h: h.tensor_scalar(S(2), a_im, dt, None, ALU.mult), [d], [d])
        self.sincos(S(2), S(3), S(4), S(5), S(6), d)
        self.dve(lambda h: h.tensor_tensor(out=S(7), in0=S(1), in1=S(4), op=ALU.mult), [d], [d])
        self.dve(lambda h: h.tensor_tensor(out=S(8), in0=S(1), in1=S(3), op=ALU.mult), [d], [d])
        self.dve(lambda h: h.tensor_scalar(S(9), S(7), -1.0, None, ALU.add), [d], [d])
        self.dve(lambda h: h.tensor_tensor(out=S(10), in0=a_re, in1=a_re, op=ALU.mult), [d], [d])
        self.dve(lambda h: h.tensor_tensor(out=S(11), in0=a_im, in1=a_im, op=ALU.mult), [d], [d])
        self.dve(lambda h: h.tensor_tensor(out=S(10), in0=S(10), in1=S(11), op=ALU.add), [d], [d])
        self.dve(lambda h: h.reciprocal(out=S(10), in_=S(10)), [d], [d])
        self.dve(lambda h: h.tensor_tensor(out=S(11), in0=S(9), in1=a_re, op=ALU.mult), [d], [d])
        self.dve(lambda h: h.tensor_tensor(out=S(12), in0=S(8), in1=a_im, op=ALU.mult), [d], [d])
        self.dve(lambda h: h.tensor_tensor(out=S(11), in0=S(11), in1=S(12), op=ALU.add), [d], [d])
        self.dve(lambda h: h.tensor_tensor(out=S(11), in0=S(11), in1=S(10), op=ALU.mult), [d], [d])
        self.dve(lambda h: h.tensor_tensor(out=S(12), in0=S(8), in1=a_re, op=ALU.mult), [d], [d])
        self.dve(lambda h: h.tensor_tensor(out=S(13), in0=S(9), in1=a_im, op=ALU.mult), [d], [d])
        self.dve(lambda h: h.tensor_tensor(out=S(12), in0=S(12), in1=S(13), op=ALU.subtract), [d], [d])
        self.dve(lambda h: h.tensor_tensor(out=S(12), in0=S(12), in1=S(10), op=ALU.mult), [d], [d])
        self.dve(lambda h: h.tensor_tensor(out=S(13), in0=S(11), in1=b_re, op=ALU.mult), [d], [d])
        self.dve(lambda h: h.tensor_tensor(out=S(14), in0=S(12), in1=b_im, op=ALU.mult), [d], [d])
        self.dve(lambda h: h.tensor_tensor(out=S(15), in0=S(13), in1=S(14), op=ALU.subtract), [d], [d])
        self.dve(lambda h: h.tensor_tensor(out=S(13), in0=S(11), in1=b_im, op=ALU.mult), [d], [d])
        self.dve(lambda h: h.tensor_tensor(out=S(14), in0=S(12), in1=b_re, op=ALU.mult), [d], [d])
        self.dve(lambda h: h.tensor_tensor(out=S(16), in0=S(13), in1=S(14), op=ALU.add), [d], [d])
        self.dve(lambda h: h.tensor_scalar(S(17), S(15), -1.0, None, ALU.mult), [d], [d])
        for j in range(8):
            mj = self.rowmask[:, j:j + 1]
            for (v, src, half) in ((0, S(15), 0), (0, S(16), 1), (1, S(16), 0), (1, S(17), 1)):
                self.dve(lambda h, v=v, src=src, half=half, j=j, mj=mj: h.tensor_scalar(
                    self.bp[:, v, j, half * 64:(half + 1) * 64], src, mj, None, ALU.mult), [d, self.kd], [d])
        cA, cB = self.ccb[:, 0, :], self.ccb[:, 1, :]
        c1 = sm[:, 18:20, :].rearrange("p a n -> p (a n)")
        c2 = sm[:, 20:22, :].rearrange("p a n -> p (a n)")
        self.dve(lambda h: h.tensor_copy(out=c1[0:64, :], in_=cA[0:64, :]), [d], [d])
        self.dve(lambda h: h.tensor_scalar(c1[64:128, :], cA[64:128, :], -1.0, None, ALU.mult), [d], [d])
        self.dve(lambda h: h.tensor_scalar(c2, cB, -1.0, None, ALU.mult), [d], [d])
        for j in range(8):
            cm = self.colmask[:, j * 128:(j + 1) * 128]
            self.dve(lambda h, j=j, cm=cm: h.tensor_tensor(out=self.cg[:, 0, j, :], in0=c1, in1=cm, op=ALU.mult), [d, self.kd], [d])
            self.dve(lambda h, j=j, cm=cm: h.tensor_tensor(out=self.cg[:, 1, j, :], in0=c2, in1=cm, op=ALU.mult), [d, self.kd], [d])
        a_re_s, a_im_s, ldt_s = self.psb[:, 0:8], self.psb[:, 8:16], self.psb[:, 16:24]
        T = lambda i: sms[:, i, :]
        P.op(P.act, lambda h: h.activation(out=T(1), in_=ldt_s, func=AF.Exp), reads=[d], writes=[d])
        self.dve(lambda h: h.tensor_tensor(out=T(2), in0=a_re_s, in1=T(1), op=ALU.mult), [d], [d])
        P.op(P.act, lambda h: h.activation(out=self.st[:, 0, :], in_=T(2), func=AF.Exp), reads=[d], writes=[d])
        self.dve(lambda h: h.tensor_tensor(out=self.st[:, 1, :], in0=a_im_s, in1=T(1), op=ALU.mult), [d], [d])
        for (n, ci) in ((TT, 2), (CTX, 4)):
            self.dve(lambda h, n=n: h.tensor_scalar(T(3), self.st[:, 1, :], float(n), None, ALU.mult), [d], [d])
            self.sincos(T(3), self.st[:, ci + 1, :], self.st[:, ci, :], T(4), T(5), d)
        td = self.tabd
        for g in range(8):
            a0, a1, a2 = self.ang[:, 0, :], self.ang[:, 1, :], self.ang[:, 2, :]
            self.dve(lambda h, g=g: h.tensor_scalar(a0, self.iota, self.st[:, 1, g:g + 1], None, ALU.mult), [d, self.kd, td], [td])
            self.sincos(a0, self.sin0[:, g, :], self.cos0[:, g, :], a1, a2, td)

    def chunk_list(self, dr):
        lat = [(CTX + i * TT, TT, False, i) for i in range(self.nch)]
        if dr == 0:
            return [(0, CTX, True, -1)] + lat
        return [(0, CTX, True, -1)] + lat[::-1]

    def rv(self, ap, n, rev):
        a = ap[:, 0:n]
        if not rev:
            return a
        return bass.AP(a.tensor, a.offset + (n - 1) * a.ap[-1][0], [list(a.ap[0]), [-a.ap[-1][0], n]])

    def scan_dir(self, bt, dr):
        P = self.P
        rev = dr == 1
        d = self.setd
        td = self.tabd
        self.dve(lambda h: h.memset(self.zinit[:], 0.0), [], [self.zid])
        chunks = self.chunk_list(dr)

        def load_u(ci):
            col0, n, is_ctx, chi = chunks[ci]
            ui = ci % 2
            P.dma(P.sp, self.uf[:, ui, 0:n], self.u_in[bt * 128:(bt + 1) * 128, col0:col0 + n], writes=[self.ufd[ui]])
            P.op(P.act, lambda h: h.activation(out=self.ub[:, ui, 0:n], in_=self.uf[:, ui, 0:n], func=AF.Copy),
                 reads=[self.ufd[ui]], writes=[self.ubd[ui]])

        def emit_x(ci, g0):
            col0, n, is_ctx, chi = chunks[ci]
            ui = ci % 2
            for g in (g0, g0 + 1):
                p1, p2 = 2 * (g % 2), 2 * (g % 2) + 1
                for v, pb in ((0, p1), (1, p2)):
                    P.mm_group(self.psum[pb][:, 0:n], [(self.bp[:, v, g, :], self.ub[:, ui, 0:n])],
                               reads=[d, self.ubd[ui]], writes=[self.psd[pb]])

        load_u(0)
        emit_x(0, 0)
        for ci, (col0, n, is_ctx, chi) in enumerate(chunks):
            for g0 in range(0, 8, 2):
                gs = (g0, g0 + 1)
                ctxs = {}
                for g in gs:
                    wi = g % 2
                    ctxs[g] = dict(p1=2 * (g % 2), p2=2 * (g % 2) + 1, wi=wi,
                                   cosT=self.rv(self.cos0[:, g, :], n, rev), sinT=self.rv(self.sin0[:, g, :], n, rev),
                                   w=self.w[:, wi, 0:n], t1=self.t1[:, wi, 0:n], zg=self.z[:, g, :])
                for g in gs:
                    c = ctxs[g]
                    self.dve(lambda h, c=c: h.tensor_tensor(out=c["t1"], in0=self.psum[c["p2"]][:, 0:n], in1=c["sinT"], op=ALU.mult),
                             [self.psd[c["p2"]], td], [self.t1d[c["wi"]]])
                    self.dve(lambda h, c=c: h.tensor_tensor(out=c["w"], in0=self.psum[c["p1"]][:, 0:n], in1=c["cosT"], op=ALU.mult),
                             [self.psd[c["p1"]], td], [self.wd[c["wi"]]])
                if g0 + 2 < 8:
                    emit_x(ci, g0 + 2)
                elif ci + 1 < len(chunks):
                    load_u(ci + 1)
                    emit_x(ci + 1, 0)
                for g in gs:
                    c = ctxs[g]
                    self.dve(lambda h, c=c: h.tensor_tensor(out=c["w"], in0=c["w"], in1=c["t1"], op=ALU.add),
                             [self.t1d[c["wi"]], self.wd[c["wi"]]], [self.wd[c["wi"]]])
                for g in gs:
                    c = ctxs[g]
                    self.dve(lambda h, g=g, c=c: h.tensor_tensor_scan(
                        out=self.rv(c["zg"], n, rev), data0=self.st[:, 0, g:g + 1].to_broadcast([128, n]),
                        data1=self.rv(c["w"], n, rev), initial=self.zinit[:, g:g + 1], op0=ALU.mult, op1=ALU.add),
                             [self.wd[c["wi"]], d, self.zid], [self.zd[g]])
                if not is_ctx:
                    for g in gs:
                        c = ctxs[g]
                        P.op(P.pool, lambda h, g=g, c=c: h.tensor_tensor(
                            out=self.Ab[:, g, 0, 0:n], in0=c["zg"][:, 0:n], in1=c["cosT"], op=ALU.mult),
                             reads=[self.zd[g], td], writes=[self.Abd[g]])
                        P.op(P.pool, lambda h, g=g, c=c: h.tensor_tensor(
                            out=self.Ab[:, g, 1, 0:n], in0=c["zg"][:, 0:n], in1=c["sinT"], op=ALU.mult),
                             reads=[self.zd[g], td], writes=[self.Abd[g]])
            last = 0 if rev else n - 1
            zl = self.z[:, :, last]
            ca, sa = (self.st[:, 2, :], self.st[:, 3, :]) if n == TT else (self.st[:, 4, :], self.st[:, 5, :])
            self.dve(lambda h, zl=zl: h.tensor_copy(out=self.zs[:, 0, :], in_=zl), self.zd, [self.zid])
            P.mm_group(self.psum[6][:, 0:8], [(self.swap, self.zs[:, 0, :])], reads=[self.zid, self.kd], writes=[self.psd[6]])
            self.dve(lambda h, sa=sa: h.tensor_tensor(out=self.zs[:, 1, :], in0=self.psum[6][:, 0:8], in1=sa, op=ALU.mult),
                     [self.psd[6], d], [self.zid])
            self.dve(lambda h, ca=ca: h.tensor_tensor(out=self.zs[:, 2, :], in0=self.zs[:, 0, :], in1=ca, op=ALU.mult),
                     [self.zid, d], [self.zid])
            self.dve(lambda h: h.tensor_tensor(out=self.zinit[:], in0=self.zs[:, 1, :], in1=self.zs[:, 2, :], op=ALU.add),
                     [self.zid], [self.zid])
            if not is_ctx:
                pb = 4 + ci % 2
                pairs = []
                for g in range(8):
                    pairs.append((self.cg[:, 0, g, :], self.Ab[:, g, 0, 0:n]))
                    pairs.append((self.cg[:, 1, g, :], self.Ab[:, g, 1, 0:n]))
                P.mm_group(self.psum[pb][:, 0:n], pairs, reads=self.Abd + [d], writes=[self.psd[pb]])
                yc = self.ybuf[:, chi * TT:(chi + 1) * TT]
                if dr == 0:
                    P.op(P.act, lambda h, yc=yc, pb=pb: h.activation(out=yc, in_=self.psum[pb][:], func=AF.Copy),
                         reads=[self.psd[pb]], writes=[self.yd[chi]])
                else:
                    self.dve(lambda h, yc=yc, pb=pb: h.tensor_tensor(out=yc, in0=yc, in1=self.psum[pb][:], op=ALU.add),
                             [self.psd[pb], self.yd[chi]], [self.yd[chi]])

    def readout(self, bt):
        P = self.P
        C0 = 0.7978845608028654
        for chi in range(self.nch):
            ui = chi % 2
            col0 = CTX + chi * TT
            P.dma(P.sp, self.uf[:, ui, :], self.u_in[bt * 128:(bt + 1) * 128, col0:col0 + TT], writes=[self.ufd[ui]])
            yc = self.ybuf[:, chi * TT:(chi + 1) * TT]
            t, t2, p, o = self.gt[:, 0, :], self.gt[:, 1, :], self.gt[:, 2, :], self.gt[:, 3, :]
            gd = self.gtd
            self.dve(lambda h, ui=ui, yc=yc: h.scalar_tensor_tensor(t, self.uf[:, ui, :], self.dk[:, bt:bt + 1], yc, ALU.mult, ALU.add),
                     [self.ufd[ui], self.yd[chi], self.kd, gd], [gd])
            self.dve(lambda h: h.tensor_tensor(out=t2, in0=t, in1=t, op=ALU.mult), [gd], [gd])
            self.dve(lambda h: h.tensor_scalar(t2, t2, 0.044715, 1.0, ALU.mult, ALU.add), [gd], [gd])
            self.dve(lambda h: h.tensor_tensor(out=p, in0=t2, in1=t, op=ALU.mult), [gd], [gd])
            P.op(P.act, lambda h: h.activation(out=p, in_=p, func=AF.Tanh, scale=C0), reads=[gd], writes=[gd])
            self.dve(lambda h: h.tensor_scalar(p, p, 1.0, 0.5, ALU.add, ALU.mult), [gd], [gd])
            self.dve(lambda h: h.tensor_tensor(out=o, in0=p, in1=t, op=ALU.mult), [gd], [gd])
            P.dma(P.sp, self.g_out[bt * 128:(bt + 1) * 128, chi * TT:(chi + 1) * TT], o, reads=[gd], writes=[self.yd[chi]],
                  is_output=True)


def host_ssm(inputs, u_full, core):
    b, q = core // 4, core % 4
    m = {"uT": np.ascontiguousarray(u_full[b, q * 512:(q + 1) * 512])}
    g0 = q * 32
    pc = np.zeros((NB_SSM, 2, 128, 257), np.float32)
    pst = np.zeros((NB_SSM, 2, 128, 24), np.float32)
    cc = np.zeros((NB_SSM, 2, 2, 128, 128), np.float32)
    for bt in range(NB_SSM):
        gs = slice(g0 + bt * 8, g0 + bt * 8 + 8)
        for dr in range(2):
            a_re = inputs["ssm_a_re"][0, dr, gs]
            a_im = inputs["ssm_a_im"][0, dr, gs]
            ldt = inputs["ssm_log_dt"][0, dr, gs]
            b_re = inputs["ssm_b_re"][0, dr, gs]
            b_im = inputs["ssm_b_im"][0, dr, gs]
            c_re = inputs["ssm_c_re"][0, dr, gs]
            c_im = inputs["ssm_c_im"][0, dr, gs]
            pc[bt, dr, :, 0:64] = np.repeat(a_re, 16, axis=0)
            pc[bt, dr, :, 64:128] = np.repeat(a_im, 16, axis=0)
            pc[bt, dr, :, 128] = np.repeat(ldt, 16)
            pc[bt, dr, :, 129:193] = b_re.transpose(0, 2, 1).reshape(128, 64)
            pc[bt, dr, :, 193:257] = b_im.transpose(0, 2, 1).reshape(128, 64)
            pst[bt, dr, :, 0:8] = np.concatenate([a_re.T, a_re.T], axis=0)
            pst[bt, dr, :, 8:16] = np.concatenate([a_im.T, a_im.T], axis=0)
            pst[bt, dr, :, 16:24] = np.broadcast_to(ldt[None, :], (128, 8))
            crT = c_re.transpose(2, 0, 1).reshape(64, 128)
            ciT = c_im.transpose(2, 0, 1).reshape(64, 128)
            cc[bt, dr, 0] = np.concatenate([crT, ciT], axis=0)
            cc[bt, dr, 1] = np.concatenate([ciT, crT], axis=0)
    m["pc"], m["pst"], m["cc"] = pc, pst, cc
    m["dk"] = np.ascontiguousarray(inputs["ssm_d"][0, q * 512:(q + 1) * 512].reshape(NB_SSM, 128).T)
    kst = np.zeros((128, 8 + 1024 + 128 + TT), np.float32)
    rows = np.arange(128)
    for j in range(8):
        kst[:, j] = (rows // 16 == j)
        kst[:, 8 + j * 128:8 + (j + 1) * 128] = (np.arange(128)[None, :] // 16 == j)
    sw = np.zeros((128, 128), np.float32)
    for mm in range(64):
        sw[64 + mm, mm] = -1.0
        sw[mm, 64 + mm] = 1.0
    kst[:, 8 + 1024:8 + 1024 + 128] = sw
    kst[:, 8 + 1024 + 128:] = np.arange(TT, dtype=np.float32)[None, :]
    m["kst"] = kst
    return m


def run_all(inputs, tpc):
    inputs = {k: np.asarray(v) for k, v in inputs.items()}
    x = inputs["x"]
    nb, seq, _ = x.shape
    ncb = seq // tpc
    ncores = nb * ncb
    assert ncores == 8
    cores = list(range(ncores))
    NJ = N_MOD * KC // 8
    mkM = MK(512, phases=["modc", "nospecial"])
    ncM = mkM.build()
    c3 = np.stack([inputs["c"][0], inputs["c"][1], inputs["c_ctx"]], axis=-1)
    cm3 = np.ascontiguousarray(c3.reshape(KC, 128, 3).transpose(1, 0, 2))
    ada_t = [tile_w(inputs["ada_w"][l]) for l in range(2)]
    adab = np.ascontiguousarray(inputs["ada_b"].reshape(2, N_MOD * KC, 128).transpose(2, 0, 1))
    mapsM = []
    for c in cores:
        sl = slice(c * NJ, (c + 1) * NJ)
        mapsM.append({"cm3": cm3, "ada0": np.ascontiguousarray(ada_t[0][sl]), "ada1": np.ascontiguousarray(ada_t[1][sl]),
                      "adab": np.ascontiguousarray(adab[:, :, sl])})
    rM = run_bass_kernel_spmd(ncM, mapsM, core_ids=cores)
    modraw = np.concatenate([rM.results[c]["modraw"] for c in cores], axis=2)
    del ada_t, mapsM
    ng = np.ascontiguousarray(inputs["norm_g"].reshape(12, KC, 128).transpose(2, 0, 1))

    def modr2(b):
        return np.ascontiguousarray(modraw[:, :, :, [b, 2]])

    mkA = MK(tpc, phases=["modd", "attn", "ffn00", "ffn01", "ffn10", "ssmu", "launchA"])
    ncA = mkA.build()
    shA = host_attn_shared(inputs)
    for (l, s) in ((0, 0), (0, 1), (1, 0)):
        shA[f"fwin{l}{s}"] = tile_w(inputs["ffn_w_in"][l, s])
        shA[f"fwout{l}{s}"] = tile_w(inputs["ffn_w_out"][l, s])
    shA["suw"] = tile_w(inputs["ssm_w_in"][0])
    shA["ng"] = ng
    mapsA = []
    for c in cores:
        m = dict(shA)
        m["xin"] = host_common(inputs, tpc, c, ncb)["xin"]
        m.update(host_attn_core(tpc, c, seq, ncb))
        m["modr2"] = modr2(c // ncb)
        mapsA.append(m)
    rA = run_bass_kernel_spmd(ncA, mapsA, core_ids=cores)
    x4 = [rA.results[c]["x4"] for c in cores]
    uo = [rA.results[c]["uTo"] for c in cores]
    del mapsA, shA
    u_full = np.zeros((nb, D, CTX + seq), np.float32)
    for b in range(nb):
        u_full[b, :, 0:CTX] = uo[b * ncb][:, tpc + 256:tpc + 512]
        for j in range(ncb):
            u_full[b, :, CTX + j * tpc:CTX + (j + 1) * tpc] = uo[b * ncb + j][:, 0:tpc]
    skB = SSMK(seq)
    ncB = skB.build()
    mapsB = [host_ssm(inputs, u_full, c) for c in cores]
    rB = run_bass_kernel_spmd(ncB, mapsB, core_ids=cores)
    g_full = np.zeros((nb, D, seq), np.float32)
    for c in cores:
        b, q = c // 4, c % 4
        g_full[b, q * 512:(q + 1) * 512] = rB.results[c]["gT"]
    del mapsB, u_full
    mkC = MK(tpc, phases=["modd", "glu", "ffn11", "yout", "nospecial", "launchC"])
    ncC = mkC.build()
    shC = {"gw": tile_w(inputs["ssm_w_glu"][0]), "fwin11": tile_w(inputs["ffn_w_in"][1, 1]),
           "fwout11": tile_w(inputs["ffn_w_out"][1, 1]), "ng": ng}
    mapsC = []
    for c in cores:
        b, j = c // ncb, c % ncb
        m = dict(shC)
        m["xin"] = np.ascontiguousarray(x4[c][:, 0:tpc])
        m["gin"] = np.ascontiguousarray(g_full[b, :, j * tpc:(j + 1) * tpc])
        m["modr2"] = modr2(b)
        mapsC.append(m)
    rC = run_bass_kernel_spmd(ncC, mapsC, core_ids=cores)
    out = np.zeros((nb, seq, D), np.float32)
    for c in cores:
        b, j = c // ncb, c % ncb
        out[b, j * tpc:(j + 1) * tpc] = rC.results[c]["yout"].T
    return out


def kernel(**inputs):
    return run_all(inputs, 4096)
```
